# Optimizing a Trainium2 kernel written in Bass

```python
import jax, jax.numpy as jnp
from jax import lax
import numpy as np

D_MODEL = 1024
BATCH = 16
SEQ = 2048
DEPTH = 1
DEC_BATCH = 16
DEC_SEQ = 64
PAST_LEN = 1024

CHUNK = 64
D_MIX = 2 * D_MODEL
D_POOL = D_MIX // 2
POOL_WINDOWS = (2, 4, 8, 16)
N_POOL_GROUPS = len(POOL_WINDOWS)
POOL_GROUP = D_POOL // N_POOL_GROUPS
POOL_HIST = max(POOL_WINDOWS) - 1
D_GLA_V = D_MIX - D_POOL
D_GLA_K = D_GLA_V // 2
N_GLA_HEADS = 4
HEAD_K = D_GLA_K // N_GLA_HEADS
HEAD_V = D_GLA_V // N_GLA_HEADS
GATE_RANK = 16
GATE_NORM = 16.0
GLA_CHUNK = CHUNK
EPS = 1e-6
D_IN_PROJ = 2 * D_POOL + 2 * D_GLA_K + 2 * D_GLA_V + GATE_RANK

kernel_name = "hybrid_pool_gla_stream_step"


def _rmsnorm(x, g):
    xf = x.astype(jnp.float32)
    y = xf * lax.rsqrt(jnp.mean(xf * xf, axis=-1, keepdims=True) + EPS)
    return y * g.astype(jnp.float32)


def _pool_mix(u_ext, pos0, w_pool, pool_scale):
    B, L, _ = u_ext.shape
    T = L - POOL_HIST
    uf = u_ext.astype(jnp.float32)
    cs = jnp.concatenate([jnp.zeros((B, 1, D_POOL), jnp.float32), jnp.cumsum(uf, axis=1)], axis=1)
    end = cs[:, POOL_HIST + 1:]
    u_t = uf[:, POOL_HIST:]
    pos = pos0 + jnp.arange(T, dtype=jnp.int32)
    parts = []
    for gi, w in enumerate(POOL_WINDOWS):
        sl = slice(gi * POOL_GROUP, (gi + 1) * POOL_GROUP)
        start = cs[:, POOL_HIST + 1 - w: POOL_HIST + 1 - w + T, sl]
        cnt = jnp.minimum(w, pos + 1).astype(jnp.float32)[None, :, None]
        parts.append((end[..., sl] - start) / cnt - u_t[..., sl])
    p = jnp.stack(parts, axis=2)
    y = jnp.einsum('btgc,gcd->btgd', p, w_pool.astype(jnp.float32)).reshape(B, T, D_POOL)
    return y * pool_scale.astype(jnp.float32)


def _gla(q, k, v, log_a, s0):
    B, T, H, _ = q.shape
    n = -(-T // GLA_CHUNK)
    pad = n * GLA_CHUNK - T

    def blocks(a):
        a = jnp.pad(a, ((0, 0), (0, pad), (0, 0), (0, 0)))
        return a.reshape(B, n, GLA_CHUNK, H, a.shape[-1]).transpose(1, 0, 3, 2, 4)

    xs = (blocks(q), blocks(k), blocks(v), blocks(log_a))
    causal = jnp.tril(jnp.ones((GLA_CHUNK, GLA_CHUNK), bool))[:, :, None]

    def step(S, inp):
        qc, kc, vc, gc = inp
        b = jnp.cumsum(gc, axis=2)
        diff = b[:, :, :, None, :] - b[:, :, None, :, :]
        decay = jnp.exp(jnp.where(causal, diff, -jnp.inf))
        scores = jnp.einsum('bhik,bhijk,bhjk->bhij', qc, decay, kc)
        o = (jnp.einsum('bhij,bhjv->bhiv', scores, vc)
             + jnp.einsum('bhik,bhkv->bhiv', qc * jnp.exp(b), S))
        b_end = b[:, :, -1, :]
        k_dec = kc * jnp.exp(b_end[:, :, None, :] - b)
        S = jnp.exp(b_end)[..., None] * S + jnp.einsum('bhjk,bhjv->bhkv', k_dec, vc)
        return S, o

    S, o = lax.scan(step, s0, xs)
    o = o.transpose(1, 0, 3, 2, 4).reshape(B, n * GLA_CHUNK, H, HEAD_V)[:, :T]
    return o, S


def _layer(x, pool_hist, gla_state, pos0, g_pre, w_in, w_gate_up, b_gate_up,
           w_pool, pool_scale, g_gla_out, w_out, g_post):
    B, T, _ = x.shape
    h = _rmsnorm(x, g_pre)
    z = h @ w_in.astype(jnp.float32)
    o1 = D_POOL
    o2 = o1 + D_POOL
    o3 = o2 + D_GLA_K
    o4 = o3 + D_GLA_K
    o5 = o4 + D_GLA_V
    o6 = o5 + D_GLA_V
    u_pool, gate_pool = z[..., :o1], z[..., o1:o2]
    q, k, v = z[..., o2:o3], z[..., o3:o4], z[..., o4:o5]
    gate_gla, g_lr = z[..., o5:o6], z[..., o6:]

    u_ext = jnp.concatenate([pool_hist.astype(jnp.float32), u_pool], axis=1)
    y_pool = _pool_mix(u_ext, pos0, w_pool, pool_scale) * jax.nn.silu(gate_pool)

    gk = g_lr @ w_gate_up.astype(jnp.float32) + b_gate_up.astype(jnp.float32)
    log_a = (jax.nn.log_sigmoid(gk) / GATE_NORM).reshape(B, T, N_GLA_HEADS, HEAD_K)
    qh = q.reshape(B, T, N_GLA_HEADS, HEAD_K) * (HEAD_K ** -0.5)
    kh = k.reshape(B, T, N_GLA_HEADS, HEAD_K)
    vh = v.reshape(B, T, N_GLA_HEADS, HEAD_V)
    o, S = _gla(qh, kh, vh, log_a, gla_state.astype(jnp.float32))
    o = _rmsnorm(o, g_gla_out).reshape(B, T, D_GLA_V)
    y_gla = o * jax.nn.silu(gate_gla)

    y = jnp.concatenate([y_pool, y_gla], axis=-1) @ w_out.astype(jnp.float32)
    x_new = (x.astype(jnp.float32) + _rmsnorm(y, g_post)).astype(x.dtype)
    new_hist = u_ext[:, -POOL_HIST:].astype(x.dtype)
    return x_new, new_hist, S.astype(x.dtype)


def setup_inputs(seed: int = 0) -> dict:
    key = jax.random.key(seed)
    ks = jax.random.split(key, 14)
    f32 = jnp.float32
    return {
        "x_prompt": jax.random.normal(ks[0], (BATCH, SEQ, D_MODEL), f32),
        "x_sample": jax.random.normal(ks[1], (DEC_BATCH, DEC_SEQ, D_MODEL), f32),
        "state_pool": jax.random.normal(ks[2], (DEPTH, DEC_BATCH, POOL_HIST, D_POOL), f32),
        "state_gla": 0.5 * jax.random.normal(ks[3], (DEPTH, DEC_BATCH, N_GLA_HEADS, HEAD_K, HEAD_V), f32),
        "g_pre": 1.0 + 0.02 * jax.random.normal(ks[4], (DEPTH, D_MODEL), f32),
        "w_in": jax.random.normal(ks[5], (DEPTH, D_MODEL, D_IN_PROJ), f32) * D_MODEL ** -0.5,
        "w_gate_up": jax.random.normal(ks[6], (DEPTH, GATE_RANK, D_GLA_K), f32) * GATE_RANK ** -0.5,
        "b_gate_up": 0.1 * jax.random.normal(ks[7], (DEPTH, D_GLA_K), f32),
        "w_pool": jax.random.normal(ks[8], (DEPTH, N_POOL_GROUPS, POOL_GROUP, POOL_GROUP), f32) * POOL_GROUP ** -0.5,
        "pool_scale": 1.0 + 0.02 * jax.random.normal(ks[9], (DEPTH, D_POOL), f32),
        "g_gla_out": 1.0 + 0.02 * jax.random.normal(ks[10], (DEPTH, HEAD_V), f32),
        "w_out": jax.random.normal(ks[11], (DEPTH, D_MIX, D_MODEL), f32) * D_MIX ** -0.5,
        "g_post": 1.0 + 0.02 * jax.random.normal(ks[12], (DEPTH, D_MODEL), f32),
    }


def reference(x_prompt, x_sample, state_pool, state_gla, g_pre, w_in, w_gate_up, b_gate_up,
              w_pool, pool_scale, g_gla_out, w_out, g_post):
    hp, hs = x_prompt, x_sample
    pool_p, gla_p, pool_s, gla_s = [], [], [], []
    for l in range(DEPTH):
        w = (g_pre[l], w_in[l], w_gate_up[l], b_gate_up[l], w_pool[l], pool_scale[l],
             g_gla_out[l], w_out[l], g_post[l])
        zero_hist = jnp.zeros((hp.shape[0], POOL_HIST, D_POOL), hp.dtype)
        zero_state = jnp.zeros((hp.shape[0], N_GLA_HEADS, HEAD_K, HEAD_V), jnp.float32)
        hp, ph, ps = _layer(hp, zero_hist, zero_state, 0, *w)
        hs, sh, ss = _layer(hs, state_pool[l], state_gla[l], PAST_LEN, *w)
        pool_p.append(ph)
        gla_p.append(ps)
        pool_s.append(sh)
        gla_s.append(ss)
    new_pool_prompt = jnp.stack(pool_p)
    new_gla_prompt = jnp.stack(gla_p)
    new_pool_sample = jnp.stack(pool_s)
    new_gla_sample = jnp.stack(gla_s)
    return (hp, hs, new_pool_prompt, new_gla_prompt, new_pool_sample, new_gla_sample)
```

```python
import numpy as np
from contextlib import ExitStack
import ml_dtypes
import concourse.bass as bass
import concourse.mybir as mybir
from concourse.bass_utils import run_bass_kernel_spmd

F32 = mybir.dt.float32
BF16 = mybir.dt.bfloat16
AF = mybir.ActivationFunctionType
ALU = mybir.AluOpType

D = 1024
DIN = 5136
NT = 33
TOK = NT * 128
EPS = 1e-6
WINDOWS = (2, 4, 8, 16)
NCB = 29
NCONST = NCB * 128 + 4


class Buf:
    def __init__(self, name):
        self.name = name
        self.writers = {}
        self.readers = {}


class Op:
    __slots__ = ("eng", "fn", "deps", "signal", "cnt", "dma")

    def __init__(self, eng, fn, deps, dma=None):
        self.eng = eng
        self.fn = fn
        self.deps = deps
        self.signal = False
        self.cnt = 0
        self.dma = dma


class Prog:
    ENGS = ("pe", "act", "dve", "pool", "sp")

    def __init__(self):
        self.q = {e: [] for e in self.ENGS}
        self.dma_cnt = {}

    def _deps(self, eng, reads, writes, is_dma):
        raw, other = set(), set()
        for b in reads:
            raw.update(b.writers.values())
        for b in writes:
            raw.update(b.writers.values())
            other.update(b.readers.values())
        deps = []
        for d in raw | other:
            if d[0] == "eng" and d[1] == eng and not is_dma:
                if eng == "pe":
                    continue
                if d not in raw:
                    continue
            deps.append(d)
        return deps

    def op(self, eng, fn, reads=(), writes=()):
        ex = [b for b in reads if b.name.startswith("bank")]
        reads = [b for b in reads if not b.name.startswith("bank")]
        writes = list(writes) + ex
        deps = self._deps(eng, reads, writes, False)
        o = Op(eng, fn, deps)
        idx = len(self.q[eng])
        self.q[eng].append(o)
        me = ("eng", eng, idx)
        for b in reads:
            b.readers[eng] = me
        for b in writes:
            b.readers = {}
            b.writers[eng] = me
        return o

    def dma(self, key, fn, reads=(), writes=()):
        deps = self._deps("sp", reads, writes, True)
        self.dma_cnt[key] = self.dma_cnt.get(key, 0) + 16
        me = ("dma", key, self.dma_cnt[key])
        o = Op("sp", fn, deps, dma=me)
        self.q["sp"].append(o)
        for b in reads:
            b.readers[("dma", key)] = me
        for b in writes:
            b.readers = {}
            b.writers[("dma", key)] = me
        return o

    def resolve(self):
        for e in self.ENGS:
            for o in self.q[e]:
                for d in o.deps:
                    if d[0] == "eng":
                        self.q[d[1]][d[2]].signal = True
        for e in self.ENGS:
            c = 0
            for o in self.q[e]:
                if o.signal:
                    c += 1
                o.cnt = c

    def emit(self, eng, handle, esem, dsem):
        waited = {}
        for o in self.q[eng]:
            for d in o.deps:
                if d[0] == "eng":
                    key = ("e", d[1])
                    sem = esem[d[1]]
                    val = self.q[d[1]][d[2]].cnt
                else:
                    key = ("d", d[1])
                    sem = dsem[d[1]]
                    val = d[2]
                if waited.get(key, 0) >= val:
                    continue
                waited[key] = val
                handle.wait_ge(sem, val)
            ins = o.fn(handle)
            if o.dma is not None:
                ins.then_inc(dsem[o.dma[1]], 16)
            elif o.signal:
                ins.then_inc(esem[eng], 1)
        if eng == "sp":
            for key, val in self.dma_cnt.items():
                if waited.get(("d", key), 0) < val:
                    handle.wait_ge(dsem[key], val)


def _build_consts():
    c = np.zeros((128, NCONST), np.float32)

    def blk(i):
        return c[:, i * 128:(i + 1) * 128]

    j = np.arange(128)[:, None]
    i = np.arange(128)[None, :]
    blk(0)[:] = np.eye(128)
    causal = (j <= i)
    same = (j // 64) == (i // 64)
    blk(1)[:] = np.where(causal, -1.0 / 16.0, 0.0)
    blk(2)[:] = np.where(causal & same, -1.0 / 16.0, 0.0)
    blk(3)[:] = np.where(causal, 1.0, 0.0)
    blk(4)[:] = np.where(causal & same, 1.0, 0.0)
    s = np.arange(128)[:, None]
    t = np.arange(128)[None, :]
    for g, w in enumerate(WINDOWS):
        inwin = (s <= t) & (s > t - w)
        cnt = np.minimum(w, t + 1).astype(np.float32)
        blk(5 + g)[:] = np.where(inwin, 1.0 / cnt, 0.0) - (s == t)
        blk(9 + g)[:] = np.where(inwin, 1.0 / w, 0.0) - (s == t)
        srel = s - 128
        blk(13 + g)[:] = np.where((s >= 64) & (srel > t - w), 1.0 / w, 0.0)
        sseg = (s // 64) == (t // 64)
        blk(17 + g)[:] = np.where(inwin & sseg, 1.0 / w, 0.0) - (s == t)
        hb = np.zeros((128, 128), np.float32)
        for seg, base in ((0, 0), (1, 32)):
            r = np.arange(15)[:, None]
            tl = np.arange(64)[None, :]
            hb[base:base + 15, seg * 64:(seg + 1) * 64] = np.where((r - 15) > tl - w, 1.0 / w, 0.0)
        blk(21 + g)[:] = hb
    c[:, NCB * 128 + 0] = blk(1)[:, 127]
    c[:, NCB * 128 + 1] = blk(1)[:, 127]
    c[:, NCB * 128 + 2] = blk(2)[:, 63]
    c[:, NCB * 128 + 3] = blk(2)[:, 127]
    for g in range(4):
        hi = blk(5 + g).astype(ml_dtypes.bfloat16).astype(np.float32)
        blk(25 + g)[:] = blk(5 + g) - hi
    return c.astype(ml_dtypes.bfloat16)


DEFAULT_CFG = dict(
    sched={1: 1, 2: 3, 4: 2, 5: 1, 9: 1, 11: 1, 12: 1},
    stageA_elem=5, stageA_pe=11, wout_late=True, win_cast=("pool",),
    sched1={1: 1, 2: 3, 4: 2, 5: 1, 9: 1, 13: 2},
    ev_hT="dve", ev_glrT="dve", ev_v=("dve", "act"), ev_u=("dve", "dve"), ev_pT="dve", ev_qkT="act", ev_ycatT="dve",
)


def build_program(nt=NT, cfg=None, dry=False):
    cfg = dict(DEFAULT_CFG, **(cfg or {}))
    nc = bass.Bass("TRN2", target_bir_lowering=False)

    def din(name, shape, dt=F32):
        return nc.dram_tensor(name, shape, dt, kind="ExternalInput").ap()

    def dout(name, shape, dt=F32):
        return nc.dram_tensor(name, shape, dt, kind="ExternalOutput").ap()

    x_d = din("x", [TOK, D])
    sp_d = din("state_pool", [2, 15, D])
    sg_d = din("state_gla", [2, 4, 128, 256])
    win_d = din("w_in", [D, DIN])
    wout_d = din("w_out", [2 * D, D])
    wpool_d = din("w_pool", [4, 256, 256])
    wup_d = din("wup", [128, 512])
    gpre_d = din("gpre", [128, 8])
    sout_d = din("sout", [128, 16])
    gpost_d = din("gpost", [128, D])
    consts_d = din("consts", [128, NCONST], BF16)
    y_d = dout("y", [TOK, D])
    pool_o = dout("pool_out", [4, 15, D])
    gla_o = dout("gla_out", [4, 4, 128, 256])

    P = Prog()
    es = ExitStack()
    with es:
        def T(name, shape, dt):
            return es.enter_context(nc.sbuf_tensor("sb_" + name, shape, dt))

        Win = T("Win", [128, 8, DIN], BF16)
        WinLR = T("WinLR", [128, 8, 128], BF16)
        Wout = T("Wout", [128, 16, D], BF16)
        Wpool = T("Wpool", [128, 4, 2, 256], BF16)
        Wup = T("Wup", [128, 512], BF16)
        consts = T("consts", [128, NCONST], BF16)
        gpost = T("gpost", [128, D], F32)
        gpre = T("gpre", [128, 8], F32)
        sout = T("sout", [128, 16], F32)
        vec = T("vec", [128, 40], F32)
        xb = [T(f"xb{i}", [128, D], F32) for i in range(3)]
        h = T("h", [128, D], BF16)
        hT = [T(f"hT{i}", [128, 8, 128], BF16) for i in range(2)]
        u = [T(f"u{i}", [128, D], BF16) for i in range(3)]
        sgp = [T(f"sgp{i}", [128, D], BF16) for i in range(2)]
        v = [T(f"v{i}", [128, D], BF16) for i in range(2)]
        sgg = [T(f"sgg{i}", [128, D], BF16) for i in range(2)]
        qk = T("qk", [128, D], BF16)
        glrT = [T(f"glrT{i}", [128, 128], BF16) for i in range(2)]
        expb = T("expb", [128, 512], F32)
        expnb = T("expnb", [128, 512], F32)
        lsp = T("lsp", [128, 512], BF16)
        qkT = T("qkT", [128, 8, 128], BF16)
        sc = T("sc", [128, 4, 128], BF16)
        ycat = T("ycat", [128, 2 * D], BF16)
        ycatT = T("ycatT", [128, 16, 128], BF16)
        t1 = T("t1", [128, D], F32)
        pT = T("pT", [128, 8, 128], BF16)
        S = [T(f"S{i}", [128, D], F32) for i in range(2)]
        Sbf = [T(f"Sbf{i}", [128, D], BF16) for i in range(2)]
        ps = es.enter_context(nc.psum_tensor("ps", [128, 8 * 512], F32))

        ufin = T("ufin", [128, D], F32)

        B = {}

        def bf(name):
            if name not in B:
                B[name] = Buf(name)
            return B[name]

        bWin, bWout, bWpool, bWup, bconsts = bf("Win"), bf("Wout"), bf("Wpool"), bf("Wup"), bf("consts")
        bgpost, bgpre, bsout = bf("gpost"), bf("gpre"), bf("sout")
        bxb = [bf(f"xb{i}") for i in range(3)]
        bh = bf("h")
        bhT = [bf(f"hT{i}") for i in range(2)]
        bu = [bf(f"u{i}") for i in range(3)]
        bsgp = [bf(f"sgp{i}") for i in range(2)]
        bv = [bf(f"v{i}") for i in range(2)]
        bsgg = [bf(f"sgg{i}") for i in range(2)]
        bqk = bf("qk")
        bglrT = [bf(f"glrT{i}") for i in range(2)]
        bexpb, bexpnb, blsp = bf("expb"), bf("expnb"), bf("lsp")
        bqkT, bsc, bycat, bycatT, bt1, bpT = bf("qkT"), bf("sc"), bf("ycat"), bf("ycatT"), bf("t1"), bf("pT")
        bufin = bf("ufin")
        bS = [bf(f"S{i}") for i in range(2)]
        bSbf = [bf(f"Sbf{i}") for i in range(2)]
        bbank = [bf(f"bank{i}") for i in range(8)]
        V_NH = 0
        V_ONE = 32
        V_SSX, V_RSX = 4, 5
        V_SSO, V_RSO = 8, 12
        V_SSY, V_RSY = 6, 7
        V_EB0 = 16
        bssx, brsx, bsso, brso, bssy, brsy = (bf("ssx"), bf("rsx"), bf("sso"), bf("rso"),
                                              bf("ssy"), bf("rsy"))
        beb2 = [bf("eb0"), bf("eb1")]
        bvconst = bf("vconst")

        def cblk(i):
            return consts[:, i * 128:(i + 1) * 128]

        ident = cblk(0)

        def bank(b, n=1):
            return ps[:, b * 512:(b + n) * 512]

        def bank_bf(b, n=1):
            return ps[:, b * 512:(b + n) * 512].bitcast(BF16)

        P.dma("c0", lambda e: e.dma_start(out=consts[:], in_=consts_d[:, :]), writes=[bconsts])
        P.dma("c1", lambda e: e.dma_start(out=gpre[:], in_=gpre_d[:, :]), writes=[bgpre])
        P.dma("c2", lambda e: e.dma_start(out=sout[:], in_=sout_d[:, :]), writes=[bsout])

        def init_vec(e):
            e.memset(vec[:, V_ONE:V_ONE + 1], 1.0)
            return e.memset(vec[:, V_NH:V_NH + 4], -0.5)
        P.op("pool", init_vec, writes=[bvconst])
        for i in range(2):
            P.op("pool", lambda e, i=i: e.memset(glrT[i][:], 0.0), writes=[bglrT[i]])
            P.op("pool", lambda e, i=i: e.memset(glrT[i][96:128, :], 1.0), writes=[bglrT[i]])
        bWing = [bf(f"Win_g{i}") for i in range(6)]
        P.op("pool", lambda e: e.memset(WinLR[:], 0.0), writes=[bWing[5]])

        ycat_f = ycat[:].bitcast(F32)
        ycatT_f = ycatT[:].rearrange("p a b -> p (a b)").bitcast(F32)
        stage_slots = [(S[0], bS[0], "sg0"), (S[1], bS[1], "sg1"), (t1, bt1, "sg2"), (gpost, bgpost, "sg3"),
                       (xb[2], bxb[2], "xl2"), (ycat_f, bycat, "sg4"), (ycatT_f, bycatT, "sg5")]
        stage_i = [0]

        def stage_piece(src_ap, rows, width, dst_ap, scale_ap, scale_buf, dst_buf, slots=None, eng=None):
            i = stage_i[0]
            stage_i[0] += 1
            slots = stage_slots if slots is None else slots
            st_t, st_b, key = slots[i % len(slots)]
            if eng is None:
                eng = cfg["win_cast"][i % len(cfg["win_cast"])]
            st_ap = st_t[0:rows, 0:width]
            P.dma(key, lambda e: e.dma_start(out=st_ap, in_=src_ap), writes=[st_b])
            rd = [st_b] + ([scale_buf] if scale_buf is not None else [])
            if eng == "pool":
                sc1 = scale_ap if scale_ap is not None else 1.0
                fn = lambda e: e.tensor_scalar(out=dst_ap, in0=st_ap, scalar1=sc1, scalar2=1.0,
                                               op0=ALU.mult, op1=ALU.mult)
            elif eng == "act":
                if scale_ap is not None:
                    fn = lambda e: e.activation(out=dst_ap, in_=st_ap, func=AF.Copy, scale=scale_ap)
                else:
                    fn = lambda e: e.activation(out=dst_ap, in_=st_ap, func=AF.Copy)
            else:
                if scale_ap is not None:
                    fn = lambda e: e.tensor_scalar(out=dst_ap, in0=st_ap, scalar1=scale_ap, scalar2=None,
                                                   op0=ALU.mult)
                else:
                    fn = lambda e: e.tensor_copy(out=dst_ap, in_=st_ap)
            P.op(eng, fn, reads=rd, writes=[dst_buf])

        def win_group(g):
            c0 = g * 1024
            width = 1024 if g < 5 else 16
            for k in range(8):
                dst = Win[:, k, c0:c0 + width] if g < 5 else WinLR[:, k, 96:112]
                stage_piece(win_d[k * 128:(k + 1) * 128, c0:c0 + width], 128, width,
                            dst, gpre[:, k:k + 1], bgpre, bWing[g])

        def loader():
            if not cfg["wout_late"]:
                for g in range(4):
                    for kk in range(2):
                        stage_piece(wpool_d[g, kk * 128:(kk + 1) * 128, :], 128, 256, Wpool[:, g, kk, :], None, None, bWpool)
            if not cfg["wout_late"]:
                for j in range(16):
                    stage_piece(wout_d[j * 128:(j + 1) * 128, :], 128, D, Wout[:, j, :], sout[:, j:j + 1], bsout, bWout)
            win_group(5)
            stage_piece(wup_d[:, :], 128, 512, Wup[:, :], None, None, bWup)
            win_group(3)
            yield
            win_group(0)
            yield
            win_group(1)
            yield
            win_group(4)
            yield
            win_group(2)
            yield
            if not cfg["wout_late"]:
                P.dma("c3", lambda e: e.dma_start(out=gpost[:], in_=gpost_d[:, :]), writes=[bgpost])
            for b in range(2):
                P.dma(f"c{6 + b}", (lambda b: lambda e: e.dma_start(
                    out=S[b][:].rearrange("p (h v) -> p h v", h=4),
                    in_=sg_d[b].rearrange("h k v -> k h v")))(b), writes=[bS[b]])
                P.op("act", (lambda b: lambda e: e.activation(out=Sbf[b][:], in_=S[b][:], func=AF.Copy))(b),
                     reads=[bS[b]], writes=[bSbf[b]])
            yield

        def wout_loader():
            late_slots = [(t1, bt1, "sg2"), (gpost, bgpost, "sg3"), (ufin, bufin, "sg6")]
            for g in range(4):
                for kk in range(2):
                    stage_piece(wpool_d[g, kk * 128:(kk + 1) * 128, :], 128, 256, Wpool[:, g, kk, :], None, None, bWpool,
                                slots=late_slots, eng="pool")
            yield
            for j in range(16):
                stage_piece(wout_d[j * 128:(j + 1) * 128, :], 128, D, Wout[:, j, :], sout[:, j:j + 1], bsout, bWout,
                            slots=late_slots, eng="pool")
                yield
            P.dma("c3", lambda e: e.dma_start(out=gpost[:], in_=gpost_d[:, :]), writes=[bgpost])
            yield

        P.op("dve", lambda e: e.memset(t1[0:64, :], 0.0), writes=[bt1])
        P.dma("c4", lambda e: e.dma_start(out=t1[0:15, :], in_=sp_d[0, :, :]), writes=[bt1])
        P.dma("c5", lambda e: e.dma_start(out=t1[32:47, :], in_=sp_d[1, :, :]), writes=[bt1])
        P.op("dve", lambda e: e.memset(u[2][:], 0.0), writes=[bu[2]])
        P.op("dve", lambda e: e.tensor_copy(out=u[2][0:64, :], in_=t1[0:64, :]), reads=[bt1], writes=[bu[2]])
        def tile_info(n):
            if n == 0:
                return dict(sample=True, first=False, last=True, segs=[(0, 0, 64), (1, 64, 128)])
            m = (n - 1) % 16
            slot = (n - 1) // 16
            return dict(sample=False, first=(m == 0), last=(m == 15), segs=[(slot, 0, 128)], seq=2 + slot)

        def load_x(n):
            slot = n % 3
            P.dma(f"xl{slot}", lambda e: e.dma_start(out=xb[slot][:], in_=x_d[n * 128:(n + 1) * 128, :]),
                  writes=[bxb[slot]])

        def stageA_elem(n):
            slot = n % 3
            xs = xb[slot]
            P.op("act", lambda e: e.activation(out=h[:], in_=xs[:], func=AF.Square, scale=1.0 / 32.0,
                                               accum_out=vec[:, V_SSX:V_SSX + 1]),
                 reads=[bxb[slot]], writes=[bh, bssx])
            P.op("pool", lambda e: e.tensor_scalar(out=vec[:, V_RSX:V_RSX + 1], in0=vec[:, V_SSX:V_SSX + 1],
                                                   scalar1=1.0, scalar2=EPS, op0=ALU.mult, op1=ALU.add),
                 reads=[bssx], writes=[brsx])
            P.op("pool", lambda e: e.tensor_tensor(out=vec[:, V_RSX:V_RSX + 1], in0=vec[:, V_RSX:V_RSX + 1],
                                                   in1=vec[:, V_NH:V_NH + 1], op=ALU.pow),
                 reads=[brsx, bvconst], writes=[brsx])
            P.op("pool", lambda e: e.tensor_scalar(out=h[:], in0=xs[:], scalar1=vec[:, V_RSX:V_RSX + 1], scalar2=1.0,
                                                   op0=ALU.mult, op1=ALU.mult),
                 reads=[bxb[slot], brsx], writes=[bh])

        def stageA_pe(n):
            hTn = hT[n % 2]

            def fn(e):
                for c in range(8):
                    ins = e.transpose(out=bank_bf(7)[:, c * 128:(c + 1) * 128], in_=h[:, c * 128:(c + 1) * 128],
                                      identity=ident)
                return ins
            P.op("pe", fn, reads=[bh, bconsts], writes=[bbank[7]])
            evac_copy(cfg["ev_hT"], hTn[:].rearrange("p a b -> p (a b)"), bank_bf(7), [bbank[7]], [bhT[n % 2]])

        def evac_copy(eng, out_ap, in_ap, reads, writes):
            if eng == "act":
                P.op("act", lambda e: e.activation(out=out_ap, in_=in_ap, func=AF.Copy), reads=reads, writes=writes)
            else:
                P.op("dve", lambda e: e.tensor_copy(out=out_ap, in_=in_ap), reads=reads, writes=writes)

        rot = [0]

        def next_bank():
            b = rot[0] % 4
            rot[0] += 1
            return b

        def colblock(n, c0, width=512):
            b = next_bank()
            hTn = hT[n % 2]

            def fn(e):
                for k in range(8):
                    ins = e.matmul(bank(b)[:, 0:width], lhsT=hTn[:, k, :], rhs=Win[:, k, c0:c0 + width],
                                   start=(k == 0), stop=(k == 7))
                return ins
            P.op("pe", fn, reads=[bhT[n % 2], bWing[c0 // 1024]], writes=[bbank[b]])
            return b

        def phase1(n):
            info = tile_info(n)
            p2 = n % 2
            Lm = cblk(2) if info["sample"] else cblk(1)
            Lend = consts[:, NCB * 128 + 2:NCB * 128 + 4] if info["sample"] else consts[:, NCB * 128:NCB * 128 + 2]
            hTn = hT[n % 2]
            b = next_bank()

            def fn_lr(e, b=b):
                for k in range(8):
                    ins = e.matmul(bank(b)[:, 0:128], lhsT=WinLR[:, k, :], rhs=hTn[:, k, :],
                                   start=(k == 0), stop=(k == 7))
                return ins
            P.op("pe", fn_lr, reads=[bhT[n % 2], bWing[5]], writes=[bbank[b]])
            evac_copy(cfg["ev_glrT"], glrT[p2][96:112, :], bank(b)[96:112, 0:128], [bbank[b]], [bglrT[p2]])
            yield
            for half in range(2):
                b = colblock(n, 3072 + half * 512)
                evac_copy(cfg["ev_v"][half], v[p2][:, half * 512:(half + 1) * 512], bank(b), [bbank[b]], [bv[p2]])
                yield
            b = next_bank()
            P.op("pe", lambda e, b=b: e.matmul(bank(b), lhsT=glrT[p2][:, :], rhs=Wup[:, :], start=True, stop=True),
                 reads=[bglrT[p2], bWup], writes=[bbank[b]])
            P.op("act", lambda e, b=b: e.activation(out=expb[:], in_=bank(b), func=AF.Exp, scale=-1.0),
                 reads=[bbank[b]], writes=[bexpb])
            P.op("act", lambda e: e.activation(out=lsp[:], in_=expb[:], func=AF.Ln, bias=1.0),
                 reads=[bexpb], writes=[blsp])
            yield
            for half in range(2):
                b = colblock(n, half * 512)
                evac_copy(cfg["ev_u"][half], u[n % 3][:, half * 512:(half + 1) * 512], bank(b), [bbank[b]], [bu[n % 3]])
                if info["last"]:
                    P.op("act", lambda e, b=b, half=half: e.activation(out=ufin[:, half * 512:(half + 1) * 512],
                                                                       in_=bank(b), func=AF.Copy),
                         reads=[bbank[b]], writes=[bufin])
                yield
            if info["last"]:
                if info["sample"]:
                    P.dma("uf", lambda e: e.dma_start(out=pool_o[0, :, :], in_=ufin[49:64, :]), reads=[bufin])
                    P.dma("uf", lambda e: e.dma_start(out=pool_o[1, :, :], in_=ufin[113:128, :]), reads=[bufin])
                else:
                    sq = info["seq"]
                    P.dma("uf", lambda e: e.dma_start(out=pool_o[sq, :, :], in_=ufin[113:128, :]), reads=[bufin])
            b = next_bank()

            def fn_bT(e, b=b):
                for hh in range(4):
                    ins = e.matmul(bank(b)[:, hh * 128:(hh + 1) * 128], lhsT=lsp[:, hh * 128:(hh + 1) * 128], rhs=Lm,
                                   start=True, stop=True)
                return ins
            P.op("pe", fn_bT, reads=[blsp, bconsts], writes=[bbank[b]])
            P.op("act", lambda e, b=b: e.activation(out=expb[:], in_=bank(b), func=AF.Exp),
                 reads=[bbank[b]], writes=[bexpb])
            P.op("act", lambda e, b=b: e.activation(out=expnb[:], in_=bank(b), func=AF.Exp, scale=-1.0),
                 reads=[bbank[b]], writes=[bexpnb])
            yield
            def fn_eb(e):
                ebv = vec[:, V_EB0 + 8 * p2:V_EB0 + 8 * p2 + 8].rearrange("p (h s) -> p h s", s=2)
                exv = expb[:].rearrange("p (h i) -> p h i", h=4)
                e.tensor_copy(out=ebv[:, :, 0:1], in_=exv[:, :, 63:64])
                return e.tensor_copy(out=ebv[:, :, 1:2], in_=exv[:, :, 127:128])
            P.op("dve", fn_eb, reads=[bexpb], writes=[beb2[p2]])
            yield
            for half in range(2):
                b = next_bank()

                def fn_gpT(e, b=b, half=half):
                    for dcl in range(4):
                        dc = half * 4 + dcl
                        for k in range(8):
                            ins = e.matmul(bank(b)[:, dcl * 128:(dcl + 1) * 128],
                                           lhsT=Win[:, k, 1024 + dc * 128:1024 + (dc + 1) * 128], rhs=hTn[:, k, :],
                                           start=(k == 0), stop=(k == 7))
                    return ins
                P.op("pe", fn_gpT, reads=[bhT[n % 2], bWing[1]], writes=[bbank[b]])
                P.op("act", lambda e, b=b, half=half: e.activation(out=sgp[p2][:, half * 512:(half + 1) * 512],
                                                                   in_=bank(b), func=AF.Silu),
                     reads=[bbank[b]], writes=[bsgp[p2]])
                yield
            for half in range(2):
                b = colblock(n, 4096 + half * 512)
                P.op("act", lambda e, b=b, half=half: e.activation(out=sgg[p2][:, half * 512:(half + 1) * 512],
                                                                   in_=bank(b), func=AF.Silu),
                     reads=[bbank[b]], writes=[bsgg[p2]])
                yield
            for which, c0 in ((0, 2048), (1, 2560)):
                b = next_bank()

                def fn_qkT(e, b=b, c0=c0):
                    for hh in range(4):
                        for k in range(8):
                            ins = e.matmul(bank(b)[:, hh * 128:(hh + 1) * 128],
                                           lhsT=Win[:, k, c0 + hh * 128:c0 + (hh + 1) * 128], rhs=hTn[:, k, :],
                                           start=(k == 0), stop=(k == 7))
                    return ins
                P.op("pe", fn_qkT, reads=[bhT[n % 2], bWing[2]], writes=[bbank[b]])
                if which == 0:
                    P.op("dve", lambda e, b=b: e.scalar_tensor_tensor(
                        out=qkT[:, 0:4, :].rearrange("p a b -> p (a b)"), in0=bank(b), scalar=float(128.0 ** -0.5),
                        in1=expb[:], op0=ALU.mult, op1=ALU.mult),
                         reads=[bbank[b], bexpb], writes=[bqkT])
                else:
                    P.op("dve", lambda e, b=b: e.tensor_tensor(
                        out=qkT[:, 4:8, :].rearrange("p a b -> p (a b)"), in0=bank(b), in1=expnb[:], op=ALU.mult),
                         reads=[bbank[b], bexpnb], writes=[bqkT])
                yield

        def chain(n):
            info = tile_info(n)
            p2 = n % 2
            un = u[n % 3]
            uprev = u[(n - 1) % 3]
            def fn_kT(e):
                for hh in range(4):
                    ins = e.transpose(out=bank_bf(4)[:, hh * 128:(hh + 1) * 128], in_=qkT[:, 4 + hh, :], identity=ident)
                return ins
            P.op("pe", fn_kT, reads=[bqkT, bconsts], writes=[bbank[4]])
            evac_copy(cfg["ev_qkT"], qk[:, 512:1024], bank_bf(4)[:, 0:512], [bbank[4]], [bqk])
            yield
            maskc = cblk(4) if info["sample"] else cblk(3)

            def fn_sc(e):
                for hh in range(4):
                    ins = e.matmul(bank(5)[:, hh * 128:(hh + 1) * 128], lhsT=qkT[:, 4 + hh, :], rhs=qkT[:, hh, :],
                                   start=True, stop=True)
                return ins
            P.op("pe", fn_sc, reads=[bqkT], writes=[bbank[5]])
            P.op("dve", lambda e: e.tensor_tensor(out=sc[:], in0=bank(5).rearrange("p (a b) -> p a b", a=4),
                                                  in1=maskc.unsqueeze(1).to_broadcast([128, 4, 128]), op=ALU.mult),
                 reads=[bbank[5], bconsts], writes=[bsc])
            yield
            def fn_o(e):
                for hh in range(4):
                    out = bank(6, 2)[:, hh * 256:(hh + 1) * 256]
                    ins = e.matmul(out, lhsT=sc[:, hh, :], rhs=v[p2][:, hh * 256:(hh + 1) * 256], start=True, stop=False)
                    nseg = len(info["segs"])
                    for si, (slot, r0, r1) in enumerate(info["segs"]):
                        ins = e.matmul(bank(6, 2)[r0:r1, hh * 256:(hh + 1) * 256], lhsT=qkT[:, hh, r0:r1],
                                       rhs=Sbf[slot][:, hh * 256:(hh + 1) * 256], start=False, stop=True)
                return ins
            rd = [bsc, bv[p2], bqkT] + [bSbf[s[0]] for s in info["segs"]]
            P.op("pe", fn_o, reads=rd, writes=[bbank[6], bbank[7]])

            def fn_sqo(e):
                for hh in range(4):
                    ins = e.activation(out=ycat[:, D + hh * 256:D + (hh + 1) * 256], in_=bank(6, 2)[:, hh * 256:(hh + 1) * 256],
                                       func=AF.Square, scale=1.0 / 16.0, accum_out=vec[:, V_SSO + hh:V_SSO + hh + 1])
                return ins
            P.op("act", fn_sqo, reads=[bbank[6], bbank[7]], writes=[bycat, bsso])
            P.op("pool", lambda e: e.tensor_scalar(out=vec[:, V_RSO:V_RSO + 4], in0=vec[:, V_SSO:V_SSO + 4],
                                                   scalar1=1.0, scalar2=EPS, op0=ALU.mult, op1=ALU.add),
                 reads=[bsso], writes=[brso])
            P.op("pool", lambda e: e.tensor_tensor(out=vec[:, V_RSO:V_RSO + 4], in0=vec[:, V_RSO:V_RSO + 4],
                                                   in1=vec[:, V_NH:V_NH + 4], op=ALU.pow),
                 reads=[brso, bvconst], writes=[brso])

            yield
            if info["sample"]:
                cur0, hist0, hrows = 17, 21, (0, 64)
            elif info["first"]:
                cur0, hist0, hrows = 5, None, None
            else:
                cur0, hist0, hrows = 9, 13, (64, 128)

            def fn_band(e):
                for c in range(8):
                    g = c // 2
                    out = bank(4, 2)[:, c * 128:(c + 1) * 128]
                    ins = e.matmul(out, lhsT=un[:, c * 128:(c + 1) * 128], rhs=cblk(cur0 + g),
                                   start=True, stop=(hist0 is None and not info["first"]))
                    if info["first"]:
                        ins = e.matmul(out, lhsT=un[:, c * 128:(c + 1) * 128], rhs=cblk(25 + g),
                                       start=False, stop=True)
                    if hist0 is not None:
                        ins = e.matmul(out, lhsT=uprev[:, c * 128:(c + 1) * 128], rhs=cblk(hist0 + g),
                                       start=False, stop=True)
                return ins
            rd = [bu[n % 3], bconsts] + ([bu[(n - 1) % 3]] if hist0 is not None else [])
            P.op("pe", fn_band, reads=rd, writes=[bbank[4], bbank[5]])
            evac_copy(cfg["ev_pT"], pT[:].rearrange("p a b -> p (a b)"), bank(4, 2), [bbank[4], bbank[5]], [bpT])
            yield
            def fn_yp(e):
                for g in range(4):
                    for dh in range(2):
                        dc = 2 * g + dh
                        for kk in range(2):
                            ins = e.matmul(bank(4, 2)[:, dc * 128:(dc + 1) * 128],
                                           lhsT=Wpool[:, g, kk, dh * 128:(dh + 1) * 128], rhs=pT[:, 2 * g + kk, :],
                                           start=(kk == 0), stop=(kk == 1))
                return ins
            P.op("pe", fn_yp, reads=[bpT, bWpool], writes=[bbank[4], bbank[5]])
            P.op("dve", lambda e: e.tensor_tensor(out=ycatT[:, 0:8, :].rearrange("p a b -> p (a b)"), in0=bank(4, 2),
                                                  in1=sgp[p2][:], op=ALU.mult),
                 reads=[bbank[4], bbank[5], bsgp[p2]], writes=[bycatT])
            yield
            def fn_yg(e):
                for hh in range(4):
                    ins = e.scalar_tensor_tensor(out=ycat[:, D + hh * 256:D + (hh + 1) * 256],
                                                 in0=bank(6, 2)[:, hh * 256:(hh + 1) * 256],
                                                 scalar=vec[:, V_RSO + hh:V_RSO + hh + 1],
                                                 in1=sgg[p2][:, hh * 256:(hh + 1) * 256], op0=ALU.mult, op1=ALU.mult)
                return ins
            P.op("dve", fn_yg, reads=[bbank[6], bbank[7], brso, bsgg[p2]], writes=[bycat])
            yield
            V_EB = V_EB0 + 8 * p2
            V_EBP = V_EB0 + 8 * (1 - p2)
            for (slot, r0, r1) in info["segs"]:
                seg = 1 if not info["sample"] else slot
                fresh = info["sample"] or info["first"]

                def fn_kv(e, r0=r0, r1=r1):
                    for hh in range(4):
                        ins = e.matmul(bank(4, 2)[:, hh * 256:(hh + 1) * 256], lhsT=qk[r0:r1, 512 + hh * 128:512 + (hh + 1) * 128],
                                       rhs=v[p2][r0:r1, hh * 256:(hh + 1) * 256], start=True, stop=True)
                    return ins
                P.op("pe", fn_kv, reads=[bqk, bv[p2]], writes=[bbank[4], bbank[5]])

                def fn_su(e, slot=slot, seg=seg, fresh=fresh):
                    for hh in range(4):
                        sc_ap = (vec[:, V_ONE:V_ONE + 1] if fresh
                                 else vec[:, V_EBP + hh * 2 + 1:V_EBP + hh * 2 + 2])
                        ins = e.scalar_tensor_tensor(out=S[slot][:, hh * 256:(hh + 1) * 256],
                                                     in0=S[slot][:, hh * 256:(hh + 1) * 256],
                                                     scalar=sc_ap,
                                                     in1=bank(4, 2)[:, hh * 256:(hh + 1) * 256], op0=ALU.mult, op1=ALU.add)
                    return ins
                P.op("dve", fn_su, reads=[bbank[4], bbank[5], beb2[1 - p2], bvconst, bS[slot]], writes=[bS[slot]])

                def fn_sb(e, slot=slot, seg=seg, last=info["last"]):
                    for hh in range(4):
                        dst = S[slot] if last else Sbf[slot]
                        ins = e.tensor_scalar(out=dst[:, hh * 256:(hh + 1) * 256], in0=S[slot][:, hh * 256:(hh + 1) * 256],
                                              scalar1=vec[:, V_EB + hh * 2 + seg:V_EB + hh * 2 + seg + 1], scalar2=None,
                                              op0=ALU.mult)
                    return ins
                if info["last"]:
                    P.op("dve", fn_sb, reads=[bS[slot], beb2[p2]], writes=[bS[slot]])
                    sq = slot if info["sample"] else info["seq"]
                    P.dma(f"ss{slot}", lambda e, slot=slot, sq=sq: e.dma_start(
                        out=gla_o[sq].rearrange("h k v -> k h v"),
                        in_=S[slot][:].rearrange("p (h v) -> p h v", h=4)), reads=[bS[slot]])
                    if info["sample"]:
                        P.op("dve", lambda e, slot=slot: e.memset(S[slot][:], 0.0), writes=[bS[slot]])
                        P.op("dve", lambda e, slot=slot: e.memset(Sbf[slot][:], 0.0), writes=[bSbf[slot]])
                else:
                    P.op("dve", fn_sb, reads=[bS[slot], beb2[p2]], writes=[bSbf[slot]])
            yield
            def fn_yT(e):
                for j in range(8):
                    ins = e.transpose(out=bank_bf(6)[:, j * 128:(j + 1) * 128], in_=ycat[:, D + j * 128:D + (j + 1) * 128],
                                      identity=ident)
                return ins
            P.op("pe", fn_yT, reads=[bycat, bconsts], writes=[bbank[6]])
            evac_copy(cfg["ev_ycatT"], ycatT[:, 8:16, :].rearrange("p a b -> p (a b)"), bank_bf(6), [bbank[6]], [bycatT])
            yield
            def fn_out(e):
                for half in range(2):
                    for j in range(16):
                        ins = e.matmul(bank(4 + half), lhsT=ycatT[:, j, :], rhs=Wout[:, j, half * 512:(half + 1) * 512],
                                       start=(j == 0), stop=(j == 15))
                return ins
            P.op("pe", fn_out, reads=[bycatT, bWout], writes=[bbank[4], bbank[5]])
            P.op("act", lambda e: e.activation(out=ycatT[:].rearrange("p a b -> p (a b)")[:, 0:D], in_=bank(4, 2),
                                               func=AF.Square, scale=1.0 / 32.0, accum_out=vec[:, V_SSY:V_SSY + 1]),
                 reads=[bbank[4], bbank[5]], writes=[bycatT, bssy])
            P.op("pool", lambda e: e.tensor_scalar(out=vec[:, V_RSY:V_RSY + 1], in0=vec[:, V_SSY:V_SSY + 1],
                                                   scalar1=1.0, scalar2=EPS, op0=ALU.mult, op1=ALU.add),
                 reads=[bssy], writes=[brsy])
            P.op("pool", lambda e: e.tensor_tensor(out=vec[:, V_RSY:V_RSY + 1], in0=vec[:, V_RSY:V_RSY + 1],
                                                   in1=vec[:, V_NH:V_NH + 1], op=ALU.pow),
                 reads=[brsy, bvconst], writes=[brsy])
            yield
            P.op("dve", lambda e: e.scalar_tensor_tensor(out=t1[:], in0=bank(4, 2), scalar=vec[:, V_RSY:V_RSY + 1],
                                                         in1=gpost[:], op0=ALU.mult, op1=ALU.mult),
                 reads=[bbank[4], bbank[5], brsy, bgpost], writes=[bt1])
            slot = n % 3
            P.op("pool", lambda e: e.tensor_tensor(out=xb[slot][:], in0=xb[slot][:], in1=t1[:], op=ALU.add),
                 reads=[bxb[slot], bt1], writes=[bxb[slot]])
            P.dma(f"xs{slot}", lambda e: e.dma_start(out=y_d[n * 128:(n + 1) * 128, :], in_=xb[slot][:]),
                  reads=[bxb[slot]])
            if n + 3 < nt:
                load_x(n + 3)
            yield

        load_x(0)

        def run_all(g):
            for _ in g:
                pass

        if True:
            SCHED = cfg["sched"]
            ld = loader()
            if nt > 1:
                load_x(1)
            next(ld)
            stageA_elem(0)
            stageA_pe(0)
            g1 = phase1(0)
            blk = 0
            for nblk in (4, 4, 2, 2, 2):
                for _ in range(nblk):
                    next(g1)
                    if blk == cfg["stageA_elem"] and nt > 1:
                        stageA_elem(1)
                    if blk == cfg["stageA_pe"] and nt > 1:
                        stageA_pe(1)
                    blk += 1
                next(ld)
            run_all(g1)
            run_all(ld)
            if nt > 2:
                load_x(2)
            prev = chain(0)
            wl = wout_loader() if cfg["wout_late"] else iter(())
            for n in range(1, nt):
                g1 = phase1(n)
                g2 = prev if prev is not None else iter(())
                SCHED = cfg["sched1"] if (n == 1 and cfg["wout_late"]) else cfg["sched"]
                blk = 0
                while True:
                    try:
                        next(g1)
                    except StopIteration:
                        break
                    for _ in range(2):
                        try:
                            next(wl)
                        except StopIteration:
                            pass
                    if blk == cfg["stageA_elem"] and n + 1 < nt:
                        stageA_elem(n + 1)
                    if blk == cfg["stageA_pe"] and n + 1 < nt:
                        stageA_pe(n + 1)
                    for _ in range(SCHED.get(blk, 0)):
                        try:
                            next(g2)
                        except StopIteration:
                            pass
                    blk += 1
                run_all(wl)
                run_all(g2)
                prev = chain(n)
            run_all(prev)

        P.resolve()
        if dry:
            return P
        esem = {e: es.enter_context(nc.semaphore(f"sem_{e}")) for e in ("pe", "act", "dve", "pool")}
        dsem = {k: es.enter_context(nc.semaphore(f"dsem_{k}")) for k in P.dma_cnt}
        with nc.Block() as block:
            @block.sync
            def _(eng):
                P.emit("sp", eng, esem, dsem)

            @block.tensor
            def _(eng):
                P.emit("pe", eng, esem, dsem)

            @block.scalar
            def _(eng):
                P.emit("act", eng, esem, dsem)

            @block.vector
            def _(eng):
                P.emit("dve", eng, esem, dsem)

            @block.gpsimd
            def _(eng):
                P.emit("pool", eng, esem, dsem)
    return nc


_CONSTS = None


def kernel(x_prompt, x_sample, state_pool, state_gla, g_pre, w_in, w_gate_up, b_gate_up,
           w_pool, pool_scale, g_gla_out, w_out, g_post):
    global _CONSTS
    if _CONSTS is None:
        _CONSTS = _build_consts()
    f = lambda a: np.ascontiguousarray(np.asarray(a, dtype=np.float32))
    x_prompt, x_sample, state_pool, state_gla = f(x_prompt), f(x_sample), f(state_pool), f(state_gla)
    g_pre, w_in, w_gate_up, b_gate_up = f(g_pre), f(w_in), f(w_gate_up), f(b_gate_up)
    w_pool, pool_scale, g_gla_out, w_out, g_post = f(w_pool), f(pool_scale), f(g_gla_out), f(w_out), f(g_post)

    wup = np.zeros((128, 512), np.float32)
    wup[96:112] = w_gate_up[0]
    wup[112] = b_gate_up[0]
    gpre = np.ascontiguousarray(g_pre[0].reshape(8, 128).T)
    srow = np.concatenate([pool_scale[0], np.tile(g_gla_out[0], 4)])
    sout = np.ascontiguousarray(srow.reshape(16, 128).T)
    gpost = np.ascontiguousarray(np.broadcast_to(g_post[0][None, :], (128, D)))

    in_maps = []
    for c in range(8):
        xs = np.concatenate([x_sample[2 * c:2 * c + 2].reshape(128, D),
                             x_prompt[2 * c].reshape(2048, D),
                             x_prompt[2 * c + 1].reshape(2048, D)], axis=0)
        in_maps.append({
            "x": np.ascontiguousarray(xs),
            "state_pool": np.ascontiguousarray(state_pool[0, 2 * c:2 * c + 2]),
            "state_gla": np.ascontiguousarray(state_gla[0, 2 * c:2 * c + 2]),
            "w_in": w_in[0], "w_out": w_out[0], "w_pool": w_pool[0], "wup": wup,
            "gpre": gpre, "sout": sout, "gpost": gpost, "consts": _CONSTS,
        })
    nc = build_program()
    res = run_bass_kernel_spmd(nc, in_maps, core_ids=list(range(8)))
    y_prompt = np.empty((16, 2048, D), np.float32)
    y_sample = np.empty((16, 64, D), np.float32)
    new_pool_prompt = np.empty((1, 16, 15, D), np.float32)
    new_gla_prompt = np.empty((1, 16, 4, 128, 256), np.float32)
    new_pool_sample = np.empty((1, 16, 15, D), np.float32)
    new_gla_sample = np.empty((1, 16, 4, 128, 256), np.float32)
    for c in range(8):
        r = res.results[c]
        y = r["y"]
        y_sample[2 * c:2 * c + 2] = y[0:128].reshape(2, 64, D)
        y_prompt[2 * c] = y[128:128 + 2048]
        y_prompt[2 * c + 1] = y[128 + 2048:128 + 4096]
        po, go = r["pool_out"], r["gla_out"]
        new_pool_sample[0, 2 * c:2 * c + 2] = po[0:2]
        new_pool_prompt[0, 2 * c:2 * c + 2] = po[2:4]
        new_gla_sample[0, 2 * c:2 * c + 2] = go[0:2]
        new_gla_prompt[0, 2 * c:2 * c + 2] = go[2:4]
    return (y_prompt, y_sample, new_pool_prompt, new_gla_prompt, new_pool_sample, new_gla_sample)
```

```python
import numpy as np
from contextlib import ExitStack
import ml_dtypes
import concourse.bass as bass
import concourse.mybir as mybir
from concourse.bass_utils import run_bass_kernel_spmd

F32 = mybir.dt.float32
BF16 = mybir.dt.bfloat16
AF = mybir.ActivationFunctionType
ALU = mybir.AluOpType

D = 1024
DIN = 5136
NT = 33
TOK = NT * 128
EPS = 1e-6
WINDOWS = (2, 4, 8, 16)
NCB = 29
NCONST = NCB * 128 + 4


class Buf:
    def __init__(self, name):
        self.name = name
        self.writers = {}
        self.readers = {}


class Op:
    __slots__ = ("eng", "fn", "deps", "signal", "cnt", "dma")

    def __init__(self, eng, fn, deps, dma=None):
        self.eng = eng
        self.fn = fn
        self.deps = deps
        self.signal = False
        self.cnt = 0
        self.dma = dma


class Prog:
    ENGS = ("pe", "act", "dve", "pool", "sp")

    def __init__(self):
        self.q = {e: [] for e in self.ENGS}
        self.dma_cnt = {}

    def _deps(self, eng, reads, writes, is_dma):
        raw, other = set(), set()
        for b in reads:
            raw.update(b.writers.values())
        for b in writes:
            raw.update(b.writers.values())
            other.update(b.readers.values())
        deps = []
        for d in raw | other:
            if d[0] == "eng" and d[1] == eng and not is_dma:
                if eng == "pe":
                    continue
                if d not in raw:
                    continue
            deps.append(d)
        return deps

    def op(self, eng, fn, reads=(), writes=()):
        ex = [b for b in reads if b.name.startswith("bank")]
        reads = [b for b in reads if not b.name.startswith("bank")]
        writes = list(writes) + ex
        deps = self._deps(eng, reads, writes, False)
        o = Op(eng, fn, deps)
        idx = len(self.q[eng])
        self.q[eng].append(o)
        me = ("eng", eng, idx)
        for b in reads:
            b.readers[eng] = me
        for b in writes:
            b.readers = {}
            b.writers[eng] = me
        return o

    def dma(self, key, fn, reads=(), writes=()):
        deps = self._deps("sp", reads, writes, True)
        self.dma_cnt[key] = self.dma_cnt.get(key, 0) + 16
        me = ("dma", key, self.dma_cnt[key])
        o = Op("sp", fn, deps, dma=me)
        self.q["sp"].append(o)
        for b in reads:
            b.readers[("dma", key)] = me
        for b in writes:
            b.readers = {}
            b.writers[("dma", key)] = me
        return o

    def resolve(self):
        for e in self.ENGS:
            for o in self.q[e]:
                for d in o.deps:
                    if d[0] == "eng":
                        self.q[d[1]][d[2]].signal = True
        for e in self.ENGS:
            c = 0
            for o in self.q[e]:
                if o.signal:
                    c += 1
                o.cnt = c

    def emit(self, eng, handle, esem, dsem):
        waited = {}
        for o in self.q[eng]:
            for d in o.deps:
                if d[0] == "eng":
                    key = ("e", d[1])
                    sem = esem[d[1]]
                    val = self.q[d[1]][d[2]].cnt
                else:
                    key = ("d", d[1])
                    sem = dsem[d[1]]
                    val = d[2]
                if waited.get(key, 0) >= val:
                    continue
                waited[key] = val
                handle.wait_ge(sem, val)
            ins = o.fn(handle)
            if o.dma is not None:
                ins.then_inc(dsem[o.dma[1]], 16)
            elif o.signal:
                ins.then_inc(esem[eng], 1)
        if eng == "sp":
            for key, val in self.dma_cnt.items():
                if waited.get(("d", key), 0) < val:
                    handle.wait_ge(dsem[key], val)


def _build_consts():
    c = np.zeros((128, NCONST), np.float32)

    def blk(i):
        return c[:, i * 128:(i + 1) * 128]

    j = np.arange(128)[:, None]
    i = np.arange(128)[None, :]
    blk(0)[:] = np.eye(128)
    causal = (j <= i)
    same = (j // 64) == (i // 64)
    blk(1)[:] = np.where(causal, -1.0 / 16.0, 0.0)
    blk(2)[:] = np.where(causal & same, -1.0 / 16.0, 0.0)
    blk(3)[:] = np.where(causal, 1.0, 0.0)
    blk(4)[:] = np.where(causal & same, 1.0, 0.0)
    s = np.arange(128)[:, None]
    t = np.arange(128)[None, :]
    for g, w in enumerate(WINDOWS):
        inwin = (s <= t) & (s > t - w)
        cnt = np.minimum(w, t + 1).astype(np.float32)
        blk(5 + g)[:] = np.where(inwin, 1.0 / cnt, 0.0) - (s == t)
        blk(9 + g)[:] = np.where(inwin, 1.0 / w, 0.0) - (s == t)
        srel = s - 128
        blk(13 + g)[:] = np.where((s >= 64) & (srel > t - w), 1.0 / w, 0.0)
        sseg = (s // 64) == (t // 64)
        blk(17 + g)[:] = np.where(inwin & sseg, 1.0 / w, 0.0) - (s == t)
        hb = np.zeros((128, 128), np.float32)
        for seg, base in ((0, 0), (1, 32)):
            r = np.arange(15)[:, None]
            tl = np.arange(64)[None, :]
            hb[base:base + 15, seg * 64:(seg + 1) * 64] = np.where((r - 15) > tl - w, 1.0 / w, 0.0)
        blk(21 + g)[:] = hb
    c[:, NCB * 128 + 0] = blk(1)[:, 127]
    c[:, NCB * 128 + 1] = blk(1)[:, 127]
    c[:, NCB * 128 + 2] = blk(2)[:, 63]
    c[:, NCB * 128 + 3] = blk(2)[:, 127]
    for g in range(4):
        hi = blk(5 + g).astype(ml_dtypes.bfloat16).astype(np.float32)
        blk(25 + g)[:] = blk(5 + g) - hi
    return c.astype(ml_dtypes.bfloat16)


DEFAULT_CFG = dict(
    sched={1: 2, 2: 2, 4: 2, 5: 1, 9: 1, 11: 1, 12: 1},
    stageA_elem=5, stageA_pe=11, wout_late=True, win_cast=("pool",),
    sched1={1: 2, 2: 2, 4: 2, 5: 1, 9: 1, 13: 2},
    ev_hT="dve", ev_glrT="dve", ev_v=("dve", "act"), ev_u=("dve", "dve"), ev_pT="dve", ev_qkT="act", ev_ycatT="dve",
)


def build_program(nt=NT, cfg=None, dry=False):
    cfg = dict(DEFAULT_CFG, **(cfg or {}))
    nc = bass.Bass("TRN2", target_bir_lowering=False)

    def din(name, shape, dt=F32):
        return nc.dram_tensor(name, shape, dt, kind="ExternalInput").ap()

    def dout(name, shape, dt=F32):
        return nc.dram_tensor(name, shape, dt, kind="ExternalOutput").ap()

    x_d = din("x", [TOK, D])
    sp_d = din("state_pool", [2, 15, D])
    sg_d = din("state_gla", [2, 4, 128, 256])
    win_d = din("w_in", [D, DIN])
    wout_d = din("w_out", [2 * D, D])
    wpool_d = din("w_pool", [4, 256, 256])
    wup_d = din("wup", [128, 512])
    gpre_d = din("gpre", [128, 8])
    sout_d = din("sout", [128, 16])
    gpost_d = din("gpost", [128, D])
    consts_d = din("consts", [128, NCONST], BF16)
    y_d = dout("y", [TOK, D])
    pool_o = dout("pool_out", [4, 15, D])
    gla_o = dout("gla_out", [4, 4, 128, 256])

    P = Prog()
    es = ExitStack()
    with es:
        def T(name, shape, dt):
            return es.enter_context(nc.sbuf_tensor("sb_" + name, shape, dt))

        Win = T("Win", [128, 8, DIN], BF16)
        WinLR = T("WinLR", [128, 8, 128], BF16)
        Wout = T("Wout", [128, 16, D], BF16)
        Wpool = T("Wpool", [128, 4, 2, 256], BF16)
        Wup = T("Wup", [128, 512], BF16)
        consts = T("consts", [128, NCONST], BF16)
        gpost = T("gpost", [128, D], F32)
        gpre = T("gpre", [128, 8], F32)
        sout = T("sout", [128, 16], F32)
        vec = T("vec", [128, 40], F32)
        xb = [T(f"xb{i}", [128, D], F32) for i in range(3)]
        h = T("h", [128, D], BF16)
        hT = [T(f"hT{i}", [128, 8, 128], BF16) for i in range(2)]
        u = [T(f"u{i}", [128, D], BF16) for i in range(3)]
        sgp = [T(f"sgp{i}", [128, D], BF16) for i in range(2)]
        v = [T(f"v{i}", [128, D], BF16) for i in range(2)]
        sgg = [T(f"sgg{i}", [128, D], BF16) for i in range(2)]
        qk = T("qk", [128, D], BF16)
        glrT = [T(f"glrT{i}", [128, 128], BF16) for i in range(2)]
        expb = T("expb", [128, 512], F32)
        expnb = T("expnb", [128, 512], F32)
        lsp = T("lsp", [128, 512], BF16)
        qkT = T("qkT", [128, 8, 128], BF16)
        sc = T("sc", [128, 4, 128], BF16)
        ycat = T("ycat", [128, 2 * D], BF16)
        ycatT = T("ycatT", [128, 16, 128], BF16)
        t1 = T("t1", [128, D], F32)
        pT = T("pT", [128, 8, 128], BF16)
        S = [T(f"S{i}", [128, D], F32) for i in range(2)]
        Sbf = [T(f"Sbf{i}", [128, D], BF16) for i in range(2)]
        ps = es.enter_context(nc.psum_tensor("ps", [128, 8 * 512], F32))

        ufin = T("ufin", [128, D], F32)

        B = {}

        def bf(name):
            if name not in B:
                B[name] = Buf(name)
            return B[name]

        bWin, bWout, bWpool, bWup, bconsts = bf("Win"), bf("Wout"), bf("Wpool"), bf("Wup"), bf("consts")
        bgpost, bgpre, bsout = bf("gpost"), bf("gpre"), bf("sout")
        bxb = [bf(f"xb{i}") for i in range(3)]
        bh = bf("h")
        bhT = [bf(f"hT{i}") for i in range(2)]
        bu = [bf(f"u{i}") for i in range(3)]
        bsgp = [bf(f"sgp{i}") for i in range(2)]
        bv = [bf(f"v{i}") for i in range(2)]
        bsgg = [bf(f"sgg{i}") for i in range(2)]
        bqk = bf("qk")
        bglrT = [bf(f"glrT{i}") for i in range(2)]
        bexpb, bexpnb, blsp = bf("expb"), bf("expnb"), bf("lsp")
        bqkT, bsc, bycat, bycatT, bt1, bpT = bf("qkT"), bf("sc"), bf("ycat"), bf("ycatT"), bf("t1"), bf("pT")
        bufin = bf("ufin")
        bS = [bf(f"S{i}") for i in range(2)]
        bSbf = [bf(f"Sbf{i}") for i in range(2)]
        bbank = [bf(f"bank{i}") for i in range(8)]
        V_NH = 0
        V_ONE = 32
        V_SSX, V_RSX = 4, 5
        V_SSO, V_RSO = 8, 12
        V_SSY, V_RSY = 6, 7
        V_EB0 = 16
        bssx, brsx, bsso, brso, bssy, brsy = (bf("ssx"), bf("rsx"), bf("sso"), bf("rso"),
                                              bf("ssy"), bf("rsy"))
        beb2 = [bf("eb0"), bf("eb1")]
        bvconst = bf("vconst")

        def cblk(i):
            return consts[:, i * 128:(i + 1) * 128]

        ident = cblk(0)

        def bank(b, n=1):
            return ps[:, b * 512:(b + n) * 512]

        def bank_bf(b, n=1):
            return ps[:, b * 512:(b + n) * 512].bitcast(BF16)

        P.dma("c0", lambda e: e.dma_start(out=consts[:], in_=consts_d[:, :]), writes=[bconsts])
        P.dma("c1", lambda e: e.dma_start(out=gpre[:], in_=gpre_d[:, :]), writes=[bgpre])
        P.dma("c2", lambda e: e.dma_start(out=sout[:], in_=sout_d[:, :]), writes=[bsout])

        def init_vec(e):
            e.memset(vec[:, V_ONE:V_ONE + 1], 1.0)
            return e.memset(vec[:, V_NH:V_NH + 4], -0.5)
        P.op("pool", init_vec, writes=[bvconst])
        for i in range(2):
            P.op("pool", lambda e, i=i: e.memset(glrT[i][:], 0.0), writes=[bglrT[i]])
            P.op("pool", lambda e, i=i: e.memset(glrT[i][96:128, :], 1.0), writes=[bglrT[i]])
        bWing = [bf(f"Win_g{i}") for i in range(6)]
        P.op("pool", lambda e: e.memset(WinLR[:], 0.0), writes=[bWing[5]])

        ycat_f = ycat[:].bitcast(F32)
        ycatT_f = ycatT[:].rearrange("p a b -> p (a b)").bitcast(F32)
        stage_slots = [(S[0], bS[0], "sg0"), (S[1], bS[1], "sg1"), (t1, bt1, "sg2"), (gpost, bgpost, "sg3"),
                       (xb[2], bxb[2], "xl2"), (ycat_f, bycat, "sg4"), (ycatT_f, bycatT, "sg5")]
        stage_i = [0]

        def stage_piece(src_ap, rows, width, dst_ap, scale_ap, scale_buf, dst_buf, slots=None, eng=None):
            i = stage_i[0]
            stage_i[0] += 1
            slots = stage_slots if slots is None else slots
            st_t, st_b, key = slots[i % len(slots)]
            if eng is None:
                eng = cfg["win_cast"][i % len(cfg["win_cast"])]
            st_ap = st_t[0:rows, 0:width]
            P.dma(key, lambda e: e.dma_start(out=st_ap, in_=src_ap), writes=[st_b])
            rd = [st_b] + ([scale_buf] if scale_buf is not None else [])
            if eng == "pool":
                sc1 = scale_ap if scale_ap is not None else 1.0
                fn = lambda e: e.tensor_scalar(out=dst_ap, in0=st_ap, scalar1=sc1, scalar2=1.0,
                                               op0=ALU.mult, op1=ALU.mult)
            elif eng == "act":
                if scale_ap is not None:
                    fn = lambda e: e.activation(out=dst_ap, in_=st_ap, func=AF.Copy, scale=scale_ap)
                else:
                    fn = lambda e: e.activation(out=dst_ap, in_=st_ap, func=AF.Copy)
            else:
                if scale_ap is not None:
                    fn = lambda e: e.tensor_scalar(out=dst_ap, in0=st_ap, scalar1=scale_ap, scalar2=None,
                                                   op0=ALU.mult)
                else:
                    fn = lambda e: e.tensor_copy(out=dst_ap, in_=st_ap)
            P.op(eng, fn, reads=rd, writes=[dst_buf])

        def win_group(g):
            c0 = g * 1024
            width = 1024 if g < 5 else 16
            for k in range(8):
                dst = Win[:, k, c0:c0 + width] if g < 5 else WinLR[:, k, 96:112]
                stage_piece(win_d[k * 128:(k + 1) * 128, c0:c0 + width], 128, width,
                            dst, gpre[:, k:k + 1], bgpre, bWing[g])

        def loader():
            if not cfg["wout_late"]:
                for g in range(4):
                    for kk in range(2):
                        stage_piece(wpool_d[g, kk * 128:(kk + 1) * 128, :], 128, 256, Wpool[:, g, kk, :], None, None, bWpool)
            if not cfg["wout_late"]:
                for j in range(16):
                    stage_piece(wout_d[j * 128:(j + 1) * 128, :], 128, D, Wout[:, j, :], sout[:, j:j + 1], bsout, bWout)
            win_group(5)
            stage_piece(wup_d[:, :], 128, 512, Wup[:, :], None, None, bWup)
            win_group(3)
            yield
            win_group(0)
            yield
            win_group(1)
            yield
            win_group(4)
            yield
            win_group(2)
            yield
            if not cfg["wout_late"]:
                P.dma("c3", lambda e: e.dma_start(out=gpost[:], in_=gpost_d[:, :]), writes=[bgpost])
            for b in range(2):
                P.dma(f"c{6 + b}", (lambda b: lambda e: e.dma_start(
                    out=S[b][:].rearrange("p (h v) -> p h v", h=4),
                    in_=sg_d[b].rearrange("h k v -> k h v")))(b), writes=[bS[b]])
                P.op("act", (lambda b: lambda e: e.activation(out=Sbf[b][:], in_=S[b][:], func=AF.Copy))(b),
                     reads=[bS[b]], writes=[bSbf[b]])
            yield

        def wout_loader():
            late_slots = [(t1, bt1, "sg2"), (gpost, bgpost, "sg3"), (ufin, bufin, "sg6")]
            for g in range(4):
                for kk in range(2):
                    stage_piece(wpool_d[g, kk * 128:(kk + 1) * 128, :], 128, 256, Wpool[:, g, kk, :], None, None, bWpool,
                                slots=late_slots, eng="pool")
            yield
            for j in range(16):
                stage_piece(wout_d[j * 128:(j + 1) * 128, :], 128, D, Wout[:, j, :], sout[:, j:j + 1], bsout, bWout,
                            slots=late_slots, eng="pool")
                yield
            P.dma("c3", lambda e: e.dma_start(out=gpost[:], in_=gpost_d[:, :]), writes=[bgpost])
            yield

        P.op("dve", lambda e: e.memset(t1[0:64, :], 0.0), writes=[bt1])
        P.dma("c4", lambda e: e.dma_start(out=t1[0:15, :], in_=sp_d[0, :, :]), writes=[bt1])
        P.dma("c5", lambda e: e.dma_start(out=t1[32:47, :], in_=sp_d[1, :, :]), writes=[bt1])
        P.op("dve", lambda e: e.memset(u[2][:], 0.0), writes=[bu[2]])
        P.op("dve", lambda e: e.tensor_copy(out=u[2][0:64, :], in_=t1[0:64, :]), reads=[bt1], writes=[bu[2]])
        def tile_info(n):
            if n == 0:
                return dict(sample=True, first=False, last=True, segs=[(0, 0, 64), (1, 64, 128)])
            m = (n - 1) % 16
            slot = (n - 1) // 16
            return dict(sample=False, first=(m == 0), last=(m == 15), segs=[(slot, 0, 128)], seq=2 + slot)

        def load_x(n):
            slot = n % 3
            P.dma(f"xl{slot}", lambda e: e.dma_start(out=xb[slot][:], in_=x_d[n * 128:(n + 1) * 128, :]),
                  writes=[bxb[slot]])

        def stageA_elem(n):
            slot = n % 3
            xs = xb[slot]
            P.op("act", lambda e: e.activation(out=h[:], in_=xs[:], func=AF.Square, scale=1.0 / 32.0,
                                               accum_out=vec[:, V_SSX:V_SSX + 1]),
                 reads=[bxb[slot]], writes=[bh, bssx])
            P.op("pool", lambda e: e.tensor_scalar(out=vec[:, V_RSX:V_RSX + 1], in0=vec[:, V_SSX:V_SSX + 1],
                                                   scalar1=1.0, scalar2=EPS, op0=ALU.mult, op1=ALU.add),
                 reads=[bssx], writes=[brsx])
            P.op("pool", lambda e: e.tensor_tensor(out=vec[:, V_RSX:V_RSX + 1], in0=vec[:, V_RSX:V_RSX + 1],
                                                   in1=vec[:, V_NH:V_NH + 1], op=ALU.pow),
                 reads=[brsx, bvconst], writes=[brsx])
            P.op("pool", lambda e: e.tensor_scalar(out=h[:], in0=xs[:], scalar1=vec[:, V_RSX:V_RSX + 1], scalar2=1.0,
                                                   op0=ALU.mult, op1=ALU.mult),
                 reads=[bxb[slot], brsx], writes=[bh])

        def stageA_pe(n):
            hTn = hT[n % 2]

            def fn(e):
                for c in range(8):
                    ins = e.transpose(out=bank_bf(7)[:, c * 128:(c + 1) * 128], in_=h[:, c * 128:(c + 1) * 128],
                                      identity=ident)
                return ins
            P.op("pe", fn, reads=[bh, bconsts], writes=[bbank[7]])
            evac_copy(cfg["ev_hT"], hTn[:].rearrange("p a b -> p (a b)"), bank_bf(7), [bbank[7]], [bhT[n % 2]])

        def evac_copy(eng, out_ap, in_ap, reads, writes):
            if eng == "act":
                P.op("act", lambda e: e.activation(out=out_ap, in_=in_ap, func=AF.Copy), reads=reads, writes=writes)
            else:
                P.op("dve", lambda e: e.tensor_copy(out=out_ap, in_=in_ap), reads=reads, writes=writes)

        rot = [0]

        def next_bank():
            b = rot[0] % 4
            rot[0] += 1
            return b

        def colblock(n, c0, width=512):
            b = next_bank()
            hTn = hT[n % 2]

            def fn(e):
                for k in range(8):
                    ins = e.matmul(bank(b)[:, 0:width], lhsT=hTn[:, k, :], rhs=Win[:, k, c0:c0 + width],
                                   start=(k == 0), stop=(k == 7))
                return ins
            P.op("pe", fn, reads=[bhT[n % 2], bWing[c0 // 1024]], writes=[bbank[b]])
            return b

        def phase1(n):
            info = tile_info(n)
            p2 = n % 2
            Lm = cblk(2) if info["sample"] else cblk(1)
            Lend = consts[:, NCB * 128 + 2:NCB * 128 + 4] if info["sample"] else consts[:, NCB * 128:NCB * 128 + 2]
            hTn = hT[n % 2]
            b = next_bank()

            def fn_lr(e, b=b):
                for k in range(8):
                    ins = e.matmul(bank(b)[:, 0:128], lhsT=WinLR[:, k, :], rhs=hTn[:, k, :],
                                   start=(k == 0), stop=(k == 7))
                return ins
            P.op("pe", fn_lr, reads=[bhT[n % 2], bWing[5]], writes=[bbank[b]])
            evac_copy(cfg["ev_glrT"], glrT[p2][96:112, :], bank(b)[96:112, 0:128], [bbank[b]], [bglrT[p2]])
            yield
            for half in range(2):
                b = colblock(n, 3072 + half * 512)
                evac_copy(cfg["ev_v"][half], v[p2][:, half * 512:(half + 1) * 512], bank(b), [bbank[b]], [bv[p2]])
                yield
            b = next_bank()
            P.op("pe", lambda e, b=b: e.matmul(bank(b), lhsT=glrT[p2][:, :], rhs=Wup[:, :], start=True, stop=True),
                 reads=[bglrT[p2], bWup], writes=[bbank[b]])
            P.op("act", lambda e, b=b: e.activation(out=expb[:], in_=bank(b), func=AF.Exp, scale=-1.0),
                 reads=[bbank[b]], writes=[bexpb])
            P.op("act", lambda e: e.activation(out=lsp[:], in_=expb[:], func=AF.Ln, bias=1.0),
                 reads=[bexpb], writes=[blsp])
            yield
            for half in range(2):
                b = colblock(n, half * 512)
                evac_copy(cfg["ev_u"][half], u[n % 3][:, half * 512:(half + 1) * 512], bank(b), [bbank[b]], [bu[n % 3]])
                if info["last"]:
                    P.op("act", lambda e, b=b, half=half: e.activation(out=ufin[:, half * 512:(half + 1) * 512],
                                                                       in_=bank(b), func=AF.Copy),
                         reads=[bbank[b]], writes=[bufin])
                yield
            if info["last"]:
                if info["sample"]:
                    P.dma("uf", lambda e: e.dma_start(out=pool_o[0, :, :], in_=ufin[49:64, :]), reads=[bufin])
                    P.dma("uf", lambda e: e.dma_start(out=pool_o[1, :, :], in_=ufin[113:128, :]), reads=[bufin])
                else:
                    sq = info["seq"]
                    P.dma("uf", lambda e: e.dma_start(out=pool_o[sq, :, :], in_=ufin[113:128, :]), reads=[bufin])
            b = next_bank()

            def fn_bT(e, b=b):
                for hh in range(4):
                    ins = e.matmul(bank(b)[:, hh * 128:(hh + 1) * 128], lhsT=lsp[:, hh * 128:(hh + 1) * 128], rhs=Lm,
                                   start=True, stop=True)
                return ins
            P.op("pe", fn_bT, reads=[blsp, bconsts], writes=[bbank[b]])
            P.op("act", lambda e, b=b: e.activation(out=expb[:], in_=bank(b), func=AF.Exp),
                 reads=[bbank[b]], writes=[bexpb])
            P.op("act", lambda e, b=b: e.activation(out=expnb[:], in_=bank(b), func=AF.Exp, scale=-1.0),
                 reads=[bbank[b]], writes=[bexpnb])
            yield
            def fn_eb(e):
                ebv = vec[:, V_EB0 + 8 * p2:V_EB0 + 8 * p2 + 8].rearrange("p (h s) -> p h s", s=2)
                exv = expb[:].rearrange("p (h i) -> p h i", h=4)
                e.tensor_copy(out=ebv[:, :, 0:1], in_=exv[:, :, 63:64])
                return e.tensor_copy(out=ebv[:, :, 1:2], in_=exv[:, :, 127:128])
            P.op("dve", fn_eb, reads=[bexpb], writes=[beb2[p2]])
            yield
            for half in range(2):
                b = next_bank()

                def fn_gpT(e, b=b, half=half):
                    for dcl in range(4):
                        dc = half * 4 + dcl
                        for k in range(8):
                            ins = e.matmul(bank(b)[:, dcl * 128:(dcl + 1) * 128],
                                           lhsT=Win[:, k, 1024 + dc * 128:1024 + (dc + 1) * 128], rhs=hTn[:, k, :],
                                           start=(k == 0), stop=(k == 7))
                    return ins
                P.op("pe", fn_gpT, reads=[bhT[n % 2], bWing[1]], writes=[bbank[b]])
                P.op("act", lambda e, b=b, half=half: e.activation(out=sgp[p2][:, half * 512:(half + 1) * 512],
                                                                   in_=bank(b), func=AF.Silu),
                     reads=[bbank[b]], writes=[bsgp[p2]])
                yield
            for half in range(2):
                b = colblock(n, 4096 + half * 512)
                P.op("act", lambda e, b=b, half=half: e.activation(out=sgg[p2][:, half * 512:(half + 1) * 512],
                                                                   in_=bank(b), func=AF.Silu),
                     reads=[bbank[b]], writes=[bsgg[p2]])
                yield
            for which, c0 in ((0, 2048), (1, 2560)):
                b = next_bank()

                def fn_qkT(e, b=b, c0=c0):
                    for hh in range(4):
                        for k in range(8):
                            ins = e.matmul(bank(b)[:, hh * 128:(hh + 1) * 128],
                                           lhsT=Win[:, k, c0 + hh * 128:c0 + (hh + 1) * 128], rhs=hTn[:, k, :],
                                           start=(k == 0), stop=(k == 7))
                    return ins
                P.op("pe", fn_qkT, reads=[bhT[n % 2], bWing[2]], writes=[bbank[b]])
                if which == 0:
                    P.op("dve", lambda e, b=b: e.scalar_tensor_tensor(
                        out=qkT[:, 0:4, :].rearrange("p a b -> p (a b)"), in0=bank(b), scalar=float(128.0 ** -0.5),
                        in1=expb[:], op0=ALU.mult, op1=ALU.mult),
                         reads=[bbank[b], bexpb], writes=[bqkT])
                else:
                    P.op("dve", lambda e, b=b: e.tensor_tensor(
                        out=qkT[:, 4:8, :].rearrange("p a b -> p (a b)"), in0=bank(b), in1=expnb[:], op=ALU.mult),
                         reads=[bbank[b], bexpnb], writes=[bqkT])
                yield

        def chain(n):
            info = tile_info(n)
            p2 = n % 2
            un = u[n % 3]
            uprev = u[(n - 1) % 3]
            def fn_kT(e):
                for hh in range(4):
                    ins = e.transpose(out=bank_bf(4)[:, hh * 128:(hh + 1) * 128], in_=qkT[:, 4 + hh, :], identity=ident)
                return ins
            P.op("pe", fn_kT, reads=[bqkT, bconsts], writes=[bbank[4]])
            evac_copy(cfg["ev_qkT"], qk[:, 512:1024], bank_bf(4)[:, 0:512], [bbank[4]], [bqk])
            yield
            maskc = cblk(4) if info["sample"] else cblk(3)

            def fn_sc(e):
                for hh in range(4):
                    ins = e.matmul(bank(5)[:, hh * 128:(hh + 1) * 128], lhsT=qkT[:, 4 + hh, :], rhs=qkT[:, hh, :],
                                   start=True, stop=True)
                return ins
            P.op("pe", fn_sc, reads=[bqkT], writes=[bbank[5]])
            P.op("dve", lambda e: e.tensor_tensor(out=sc[:], in0=bank(5).rearrange("p (a b) -> p a b", a=4),
                                                  in1=maskc.unsqueeze(1).to_broadcast([128, 4, 128]), op=ALU.mult),
                 reads=[bbank[5], bconsts], writes=[bsc])
            yield
            def fn_o(e):
                for hh in range(4):
                    out = bank(6, 2)[:, hh * 256:(hh + 1) * 256]
                    ins = e.matmul(out, lhsT=sc[:, hh, :], rhs=v[p2][:, hh * 256:(hh + 1) * 256], start=True, stop=False)
                    nseg = len(info["segs"])
                    for si, (slot, r0, r1) in enumerate(info["segs"]):
                        ins = e.matmul(bank(6, 2)[r0:r1, hh * 256:(hh + 1) * 256], lhsT=qkT[:, hh, r0:r1],
                                       rhs=Sbf[slot][:, hh * 256:(hh + 1) * 256], start=False, stop=True)
                return ins
            rd = [bsc, bv[p2], bqkT] + [bSbf[s[0]] for s in info["segs"]]
            P.op("pe", fn_o, reads=rd, writes=[bbank[6], bbank[7]])

            def fn_sqo(e):
                for hh in range(4):
                    ins = e.activation(out=ycat[:, D + hh * 256:D + (hh + 1) * 256], in_=bank(6, 2)[:, hh * 256:(hh + 1) * 256],
                                       func=AF.Square, scale=1.0 / 16.0, accum_out=vec[:, V_SSO + hh:V_SSO + hh + 1])
                return ins
            P.op("act", fn_sqo, reads=[bbank[6], bbank[7]], writes=[bycat, bsso])
            P.op("pool", lambda e: e.tensor_scalar(out=vec[:, V_RSO:V_RSO + 4], in0=vec[:, V_SSO:V_SSO + 4],
                                                   scalar1=1.0, scalar2=EPS, op0=ALU.mult, op1=ALU.add),
                 reads=[bsso], writes=[brso])
            P.op("pool", lambda e: e.tensor_tensor(out=vec[:, V_RSO:V_RSO + 4], in0=vec[:, V_RSO:V_RSO + 4],
                                                   in1=vec[:, V_NH:V_NH + 4], op=ALU.pow),
                 reads=[brso, bvconst], writes=[brso])

            yield
            if info["sample"]:
                cur0, hist0, hrows = 17, 21, (0, 64)
            elif info["first"]:
                cur0, hist0, hrows = 5, None, None
            else:
                cur0, hist0, hrows = 9, 13, (64, 128)

            def fn_band(e):
                for c in range(8):
                    g = c // 2
                    out = bank(4, 2)[:, c * 128:(c + 1) * 128]
                    ins = e.matmul(out, lhsT=un[:, c * 128:(c + 1) * 128], rhs=cblk(cur0 + g),
                                   start=True, stop=(hist0 is None and not info["first"]))
                    if info["first"]:
                        ins = e.matmul(out, lhsT=un[:, c * 128:(c + 1) * 128], rhs=cblk(25 + g),
                                       start=False, stop=True)
                    if hist0 is not None:
                        ins = e.matmul(out, lhsT=uprev[:, c * 128:(c + 1) * 128], rhs=cblk(hist0 + g),
                                       start=False, stop=True)
                return ins
            rd = [bu[n % 3], bconsts] + ([bu[(n - 1) % 3]] if hist0 is not None else [])
            P.op("pe", fn_band, reads=rd, writes=[bbank[4], bbank[5]])
            evac_copy(cfg["ev_pT"], pT[:].rearrange("p a b -> p (a b)"), bank(4, 2), [bbank[4], bbank[5]], [bpT])
            yield
            def fn_yp(e):
                for g in range(4):
                    for dh in range(2):
                        dc = 2 * g + dh
                        for kk in range(2):
                            ins = e.matmul(bank(4, 2)[:, dc * 128:(dc + 1) * 128],
                                           lhsT=Wpool[:, g, kk, dh * 128:(dh + 1) * 128], rhs=pT[:, 2 * g + kk, :],
                                           start=(kk == 0), stop=(kk == 1))
                return ins
            P.op("pe", fn_yp, reads=[bpT, bWpool], writes=[bbank[4], bbank[5]])
            P.op("dve", lambda e: e.tensor_tensor(out=ycatT[:, 0:8, :].rearrange("p a b -> p (a b)"), in0=bank(4, 2),
                                                  in1=sgp[p2][:], op=ALU.mult),
                 reads=[bbank[4], bbank[5], bsgp[p2]], writes=[bycatT])
            yield
            def fn_yg(e):
                for hh in range(4):
                    ins = e.scalar_tensor_tensor(out=ycat[:, D + hh * 256:D + (hh + 1) * 256],
                                                 in0=bank(6, 2)[:, hh * 256:(hh + 1) * 256],
                                                 scalar=vec[:, V_RSO + hh:V_RSO + hh + 1],
                                                 in1=sgg[p2][:, hh * 256:(hh + 1) * 256], op0=ALU.mult, op1=ALU.mult)
                return ins
            P.op("dve", fn_yg, reads=[bbank[6], bbank[7], brso, bsgg[p2]], writes=[bycat])
            yield
            V_EB = V_EB0 + 8 * p2
            V_EBP = V_EB0 + 8 * (1 - p2)
            for (slot, r0, r1) in info["segs"]:
                seg = 1 if not info["sample"] else slot
                fresh = info["sample"] or info["first"]

                def fn_kv(e, r0=r0, r1=r1):
                    for hh in range(4):
                        ins = e.matmul(bank(4, 2)[:, hh * 256:(hh + 1) * 256], lhsT=qk[r0:r1, 512 + hh * 128:512 + (hh + 1) * 128],
                                       rhs=v[p2][r0:r1, hh * 256:(hh + 1) * 256], start=True, stop=True)
                    return ins
                P.op("pe", fn_kv, reads=[bqk, bv[p2]], writes=[bbank[4], bbank[5]])

                def fn_su(e, slot=slot, seg=seg, fresh=fresh):
                    for hh in range(4):
                        sc_ap = (vec[:, V_ONE:V_ONE + 1] if fresh
                                 else vec[:, V_EBP + hh * 2 + 1:V_EBP + hh * 2 + 2])
                        ins = e.scalar_tensor_tensor(out=S[slot][:, hh * 256:(hh + 1) * 256],
                                                     in0=S[slot][:, hh * 256:(hh + 1) * 256],
                                                     scalar=sc_ap,
                                                     in1=bank(4, 2)[:, hh * 256:(hh + 1) * 256], op0=ALU.mult, op1=ALU.add)
                    return ins
                P.op("dve", fn_su, reads=[bbank[4], bbank[5], beb2[1 - p2], bvconst, bS[slot]], writes=[bS[slot]])

                def fn_sb(e, slot=slot, seg=seg, last=info["last"]):
                    for hh in range(4):
                        dst = S[slot] if last else Sbf[slot]
                        ins = e.tensor_scalar(out=dst[:, hh * 256:(hh + 1) * 256], in0=S[slot][:, hh * 256:(hh + 1) * 256],
                                              scalar1=vec[:, V_EB + hh * 2 + seg:V_EB + hh * 2 + seg + 1], scalar2=None,
                                              op0=ALU.mult)
                    return ins
                if info["last"]:
                    P.op("dve", fn_sb, reads=[bS[slot], beb2[p2]], writes=[bS[slot]])
                    sq = slot if info["sample"] else info["seq"]
                    P.dma(f"ss{slot}", lambda e, slot=slot, sq=sq: e.dma_start(
                        out=gla_o[sq].rearrange("h k v -> k h v"),
                        in_=S[slot][:].rearrange("p (h v) -> p h v", h=4)), reads=[bS[slot]])
                    if info["sample"]:
                        P.op("dve", lambda e, slot=slot: e.memset(S[slot][:], 0.0), writes=[bS[slot]])
                        P.op("dve", lambda e, slot=slot: e.memset(Sbf[slot][:], 0.0), writes=[bSbf[slot]])
                else:
                    P.op("dve", fn_sb, reads=[bS[slot], beb2[p2]], writes=[bSbf[slot]])
            yield
            def fn_yT(e):
                for j in range(8):
                    ins = e.transpose(out=bank_bf(6)[:, j * 128:(j + 1) * 128], in_=ycat[:, D + j * 128:D + (j + 1) * 128],
                                      identity=ident)
                return ins
            P.op("pe", fn_yT, reads=[bycat, bconsts], writes=[bbank[6]])
            evac_copy(cfg["ev_ycatT"], ycatT[:, 8:16, :].rearrange("p a b -> p (a b)"), bank_bf(6), [bbank[6]], [bycatT])
            yield
            def fn_out(e):
                for half in range(2):
                    for j in range(16):
                        ins = e.matmul(bank(4 + half), lhsT=ycatT[:, j, :], rhs=Wout[:, j, half * 512:(half + 1) * 512],
                                       start=(j == 0), stop=(j == 15))
                return ins
            P.op("pe", fn_out, reads=[bycatT, bWout], writes=[bbank[4], bbank[5]])
            P.op("act", lambda e: e.activation(out=ycatT[:].rearrange("p a b -> p (a b)")[:, 0:D], in_=bank(4, 2),
                                               func=AF.Square, scale=1.0 / 32.0, accum_out=vec[:, V_SSY:V_SSY + 1]),
                 reads=[bbank[4], bbank[5]], writes=[bycatT, bssy])
            P.op("pool", lambda e: e.tensor_scalar(out=vec[:, V_RSY:V_RSY + 1], in0=vec[:, V_SSY:V_SSY + 1],
                                                   scalar1=1.0, scalar2=EPS, op0=ALU.mult, op1=ALU.add),
                 reads=[bssy], writes=[brsy])
            P.op("pool", lambda e: e.tensor_tensor(out=vec[:, V_RSY:V_RSY + 1], in0=vec[:, V_RSY:V_RSY + 1],
                                                   in1=vec[:, V_NH:V_NH + 1], op=ALU.pow),
                 reads=[brsy, bvconst], writes=[brsy])
            yield
            P.op("dve", lambda e: e.scalar_tensor_tensor(out=t1[:], in0=bank(4, 2), scalar=vec[:, V_RSY:V_RSY + 1],
                                                         in1=gpost[:], op0=ALU.mult, op1=ALU.mult),
                 reads=[bbank[4], bbank[5], brsy, bgpost], writes=[bt1])
            slot = n % 3
            P.op("pool", lambda e: e.tensor_tensor(out=xb[slot][:], in0=xb[slot][:], in1=t1[:], op=ALU.add),
                 reads=[bxb[slot], bt1], writes=[bxb[slot]])
            P.dma(f"xs{slot}", lambda e: e.dma_start(out=y_d[n * 128:(n + 1) * 128, :], in_=xb[slot][:]),
                  reads=[bxb[slot]])
            if n + 3 < nt:
                load_x(n + 3)
            yield

        load_x(0)

        def run_all(g):
            for _ in g:
                pass

        if True:
            SCHED = cfg["sched"]
            ld = loader()
            if nt > 1:
                load_x(1)
            next(ld)
            stageA_elem(0)
            stageA_pe(0)
            g1 = phase1(0)
            blk = 0
            for nblk in (4, 4, 2, 2, 2):
                for _ in range(nblk):
                    next(g1)
                    if blk == cfg["stageA_elem"] and nt > 1:
                        stageA_elem(1)
                    if blk == cfg["stageA_pe"] and nt > 1:
                        stageA_pe(1)
                    blk += 1
                next(ld)
            run_all(g1)
            run_all(ld)
            if nt > 2:
                load_x(2)
            prev = chain(0)
            wl = wout_loader() if cfg["wout_late"] else iter(())
            for n in range(1, nt):
                g1 = phase1(n)
                g2 = prev if prev is not None else iter(())
                SCHED = cfg["sched1"] if (n == 1 and cfg["wout_late"]) else cfg["sched"]
                blk = 0
                while True:
                    try:
                        next(g1)
                    except StopIteration:
                        break
                    for _ in range(2):
                        try:
                            next(wl)
                        except StopIteration:
                            pass
                    if blk == cfg["stageA_elem"] and n + 1 < nt:
                        stageA_elem(n + 1)
                    if blk == cfg["stageA_pe"] and n + 1 < nt:
                        stageA_pe(n + 1)
                    for _ in range(SCHED.get(blk, 0)):
                        try:
                            next(g2)
                        except StopIteration:
                            pass
                    blk += 1
                run_all(wl)
                run_all(g2)
                prev = chain(n)
            run_all(prev)

        P.resolve()
        if dry:
            return P
        esem = {e: es.enter_context(nc.semaphore(f"sem_{e}")) for e in ("pe", "act", "dve", "pool")}
        dsem = {k: es.enter_context(nc.semaphore(f"dsem_{k}")) for k in P.dma_cnt}
        with nc.Block() as block:
            @block.sync
            def _(eng):
                P.emit("sp", eng, esem, dsem)

            @block.tensor
            def _(eng):
                P.emit("pe", eng, esem, dsem)

            @block.scalar
            def _(eng):
                P.emit("act", eng, esem, dsem)

            @block.vector
            def _(eng):
                P.emit("dve", eng, esem, dsem)

            @block.gpsimd
            def _(eng):
                P.emit("pool", eng, esem, dsem)
    return nc


_CONSTS = None


def kernel(x_prompt, x_sample, state_pool, state_gla, g_pre, w_in, w_gate_up, b_gate_up,
           w_pool, pool_scale, g_gla_out, w_out, g_post):
    global _CONSTS
    if _CONSTS is None:
        _CONSTS = _build_consts()
    f = lambda a: np.ascontiguousarray(np.asarray(a, dtype=np.float32))
    x_prompt, x_sample, state_pool, state_gla = f(x_prompt), f(x_sample), f(state_pool), f(state_gla)
    g_pre, w_in, w_gate_up, b_gate_up = f(g_pre), f(w_in), f(w_gate_up), f(b_gate_up)
    w_pool, pool_scale, g_gla_out, w_out, g_post = f(w_pool), f(pool_scale), f(g_gla_out), f(w_out), f(g_post)

    wup = np.zeros((128, 512), np.float32)
    wup[96:112] = w_gate_up[0]
    wup[112] = b_gate_up[0]
    gpre = np.ascontiguousarray(g_pre[0].reshape(8, 128).T)
    srow = np.concatenate([pool_scale[0], np.tile(g_gla_out[0], 4)])
    sout = np.ascontiguousarray(srow.reshape(16, 128).T)
    gpost = np.ascontiguousarray(np.broadcast_to(g_post[0][None, :], (128, D)))

    in_maps = []
    for c in range(8):
        xs = np.concatenate([x_sample[2 * c:2 * c + 2].reshape(128, D),
                             x_prompt[2 * c].reshape(2048, D),
                             x_prompt[2 * c + 1].reshape(2048, D)], axis=0)
        in_maps.append({
            "x": np.ascontiguousarray(xs),
            "state_pool": np.ascontiguousarray(state_pool[0, 2 * c:2 * c + 2]),
            "state_gla": np.ascontiguousarray(state_gla[0, 2 * c:2 * c + 2]),
            "w_in": w_in[0], "w_out": w_out[0], "w_pool": w_pool[0], "wup": wup,
            "gpre": gpre, "sout": sout, "gpost": gpost, "consts": _CONSTS,
        })
    nc = build_program()
    res = run_bass_kernel_spmd(nc, in_maps, core_ids=list(range(8)))
    y_prompt = np.empty((16, 2048, D), np.float32)
    y_sample = np.empty((16, 64, D), np.float32)
    new_pool_prompt = np.empty((1, 16, 15, D), np.float32)
    new_gla_prompt = np.empty((1, 16, 4, 128, 256), np.float32)
    new_pool_sample = np.empty((1, 16, 15, D), np.float32)
    new_gla_sample = np.empty((1, 16, 4, 128, 256), np.float32)
    for c in range(8):
        r = res.results[c]
        y = r["y"]
        y_sample[2 * c:2 * c + 2] = y[0:128].reshape(2, 64, D)
        y_prompt[2 * c] = y[128:128 + 2048]
        y_prompt[2 * c + 1] = y[128 + 2048:128 + 4096]
        po, go = r["pool_out"], r["gla_out"]
        new_pool_sample[0, 2 * c:2 * c + 2] = po[0:2]
        new_pool_prompt[0, 2 * c:2 * c + 2] = po[2:4]
        new_gla_sample[0, 2 * c:2 * c + 2] = go[0:2]
        new_gla_prompt[0, 2 * c:2 * c + 2] = go[2:4]
    return (y_prompt, y_sample, new_pool_prompt, new_gla_prompt, new_pool_sample, new_gla_sample)
```

```python
import numpy as np
from contextlib import ExitStack
import ml_dtypes
import concourse.bass as bass
import concourse.mybir as mybir
from concourse.bass_utils import run_bass_kernel_spmd

F32 = mybir.dt.float32
BF16 = mybir.dt.bfloat16
AF = mybir.ActivationFunctionType
ALU = mybir.AluOpType

D = 1024
DIN = 5136
NT = 33
TOK = NT * 128
EPS = 1e-6
WINDOWS = (2, 4, 8, 16)
NCB = 29
NCONST = NCB * 128 + 4


class Buf:
    def __init__(self, name):
        self.name = name
        self.writers = {}
        self.readers = {}


class Op:
    __slots__ = ("eng", "fn", "deps", "signal", "cnt", "dma")

    def __init__(self, eng, fn, deps, dma=None):
        self.eng = eng
        self.fn = fn
        self.deps = deps
        self.signal = False
        self.cnt = 0
        self.dma = dma


class Prog:
    ENGS = ("pe", "act", "dve", "pool", "sp")

    def __init__(self):
        self.q = {e: [] for e in self.ENGS}
        self.dma_cnt = {}

    def _deps(self, eng, reads, writes, is_dma):
        raw, other = set(), set()
        for b in reads:
            raw.update(b.writers.values())
        for b in writes:
            raw.update(b.writers.values())
            other.update(b.readers.values())
        deps = []
        for d in raw | other:
            if d[0] == "eng" and d[1] == eng and not is_dma:
                if eng == "pe":
                    continue
                if d not in raw:
                    continue
            deps.append(d)
        return deps

    def op(self, eng, fn, reads=(), writes=()):
        ex = [b for b in reads if b.name.startswith("bank")]
        reads = [b for b in reads if not b.name.startswith("bank")]
        writes = list(writes) + ex
        deps = self._deps(eng, reads, writes, False)
        o = Op(eng, fn, deps)
        idx = len(self.q[eng])
        self.q[eng].append(o)
        me = ("eng", eng, idx)
        for b in reads:
            b.readers[eng] = me
        for b in writes:
            b.readers = {}
            b.writers[eng] = me
        return o

    def dma(self, key, fn, reads=(), writes=()):
        deps = self._deps("sp", reads, writes, True)
        self.dma_cnt[key] = self.dma_cnt.get(key, 0) + 16
        me = ("dma", key, self.dma_cnt[key])
        o = Op("sp", fn, deps, dma=me)
        self.q["sp"].append(o)
        for b in reads:
            b.readers[("dma", key)] = me
        for b in writes:
            b.readers = {}
            b.writers[("dma", key)] = me
        return o

    def resolve(self):
        for e in self.ENGS:
            for o in self.q[e]:
                for d in o.deps:
                    if d[0] == "eng":
                        self.q[d[1]][d[2]].signal = True
        for e in self.ENGS:
            c = 0
            for o in self.q[e]:
                if o.signal:
                    c += 1
                o.cnt = c

    def emit(self, eng, handle, esem, dsem):
        waited = {}
        for o in self.q[eng]:
            for d in o.deps:
                if d[0] == "eng":
                    key = ("e", d[1])
                    sem = esem[d[1]]
                    val = self.q[d[1]][d[2]].cnt
                else:
                    key = ("d", d[1])
                    sem = dsem[d[1]]
                    val = d[2]
                if waited.get(key, 0) >= val:
                    continue
                waited[key] = val
                handle.wait_ge(sem, val)
            ins = o.fn(handle)
            if o.dma is not None:
                ins.then_inc(dsem[o.dma[1]], 16)
            elif o.signal:
                ins.then_inc(esem[eng], 1)
        if eng == "sp":
            for key, val in self.dma_cnt.items():
                if waited.get(("d", key), 0) < val:
                    handle.wait_ge(dsem[key], val)


def _build_consts():
    c = np.zeros((128, NCONST), np.float32)

    def blk(i):
        return c[:, i * 128:(i + 1) * 128]

    j = np.arange(128)[:, None]
    i = np.arange(128)[None, :]
    blk(0)[:] = np.eye(128)
    causal = (j <= i)
    same = (j // 64) == (i // 64)
    blk(1)[:] = np.where(causal, -1.0 / 16.0, 0.0)
    blk(2)[:] = np.where(causal & same, -1.0 / 16.0, 0.0)
    blk(3)[:] = np.where(causal, 1.0, 0.0)
    blk(4)[:] = np.where(causal & same, 1.0, 0.0)
    s = np.arange(128)[:, None]
    t = np.arange(128)[None, :]
    for g, w in enumerate(WINDOWS):
        inwin = (s <= t) & (s > t - w)
        cnt = np.minimum(w, t + 1).astype(np.float32)
        blk(5 + g)[:] = np.where(inwin, 1.0 / cnt, 0.0) - (s == t)
        blk(9 + g)[:] = np.where(inwin, 1.0 / w, 0.0) - (s == t)
        srel = s - 128
        blk(13 + g)[:] = np.where((s >= 64) & (srel > t - w), 1.0 / w, 0.0)
        sseg = (s // 64) == (t // 64)
        blk(17 + g)[:] = np.where(inwin & sseg, 1.0 / w, 0.0) - (s == t)
        hb = np.zeros((128, 128), np.float32)
        for seg, base in ((0, 0), (1, 32)):
            r = np.arange(15)[:, None]
            tl = np.arange(64)[None, :]
            hb[base:base + 15, seg * 64:(seg + 1) * 64] = np.where((r - 15) > tl - w, 1.0 / w, 0.0)
        blk(21 + g)[:] = hb
    c[:, NCB * 128 + 0] = blk(1)[:, 127]
    c[:, NCB * 128 + 1] = blk(1)[:, 127]
    c[:, NCB * 128 + 2] = blk(2)[:, 63]
    c[:, NCB * 128 + 3] = blk(2)[:, 127]
    for g in range(4):
        hi = blk(5 + g).astype(ml_dtypes.bfloat16).astype(np.float32)
        blk(25 + g)[:] = blk(5 + g) - hi
    return c.astype(ml_dtypes.bfloat16)


DEFAULT_CFG = dict(
    sched={1: 2, 2: 2, 4: 2, 5: 1, 9: 1, 11: 1, 12: 1},
    stageA_elem=5, stageA_pe=11, wout_late=True, win_cast=("pool",),
    sched1={1: 2, 2: 2, 4: 2, 5: 1, 9: 1},
    defer_out0=True, carry_blk=0, stageA_elem2=5,
    ev_hT="dve", ev_glrT="dve", ev_v=("dve", "act"), ev_u=("dve", "dve"), ev_pT="dve", ev_qkT="act", ev_ycatT="dve",
)


def build_program(nt=NT, cfg=None, dry=False):
    cfg = dict(DEFAULT_CFG, **(cfg or {}))
    nc = bass.Bass("TRN2", target_bir_lowering=False)

    def din(name, shape, dt=F32):
        return nc.dram_tensor(name, shape, dt, kind="ExternalInput").ap()

    def dout(name, shape, dt=F32):
        return nc.dram_tensor(name, shape, dt, kind="ExternalOutput").ap()

    x_d = din("x", [TOK, D])
    sp_d = din("state_pool", [2, 15, D])
    sg_d = din("state_gla", [2, 4, 128, 256])
    win_d = din("w_in", [D, DIN])
    wout_d = din("w_out", [2 * D, D])
    wpool_d = din("w_pool", [4, 256, 256])
    wup_d = din("wup", [128, 512])
    gpre_d = din("gpre", [128, 8])
    sout_d = din("sout", [128, 16])
    gpost_d = din("gpost", [128, D])
    consts_d = din("consts", [128, NCONST], BF16)
    y_d = dout("y", [TOK, D])
    pool_o = dout("pool_out", [4, 15, D])
    gla_o = dout("gla_out", [4, 4, 128, 256])

    P = Prog()
    es = ExitStack()
    with es:
        def T(name, shape, dt):
            return es.enter_context(nc.sbuf_tensor("sb_" + name, shape, dt))

        Win = T("Win", [128, 8, DIN], BF16)
        WinLR = T("WinLR", [128, 8, 128], BF16)
        Wout = T("Wout", [128, 16, D], BF16)
        Wpool = T("Wpool", [128, 4, 2, 256], BF16)
        Wup = T("Wup", [128, 512], BF16)
        consts = T("consts", [128, NCONST], BF16)
        gpost = T("gpost", [128, D], F32)
        gpre = T("gpre", [128, 8], F32)
        sout = T("sout", [128, 16], F32)
        vec = T("vec", [128, 40], F32)
        xb = [T(f"xb{i}", [128, D], F32) for i in range(3)]
        h = T("h", [128, D], BF16)
        hT = [T(f"hT{i}", [128, 8, 128], BF16) for i in range(2)]
        u = [T(f"u{i}", [128, D], BF16) for i in range(3)]
        sgp = [T(f"sgp{i}", [128, D], BF16) for i in range(2)]
        v = [T(f"v{i}", [128, D], BF16) for i in range(2)]
        sgg = [T(f"sgg{i}", [128, D], BF16) for i in range(2)]
        qk = T("qk", [128, D], BF16)
        glrT = [T(f"glrT{i}", [128, 128], BF16) for i in range(2)]
        expb = T("expb", [128, 512], F32)
        expnb = T("expnb", [128, 512], F32)
        lsp = T("lsp", [128, 512], BF16)
        qkT = T("qkT", [128, 8, 128], BF16)
        sc = T("sc", [128, 4, 128], BF16)
        ycat = T("ycat", [128, 2 * D], BF16)
        ycatT = T("ycatT", [128, 16, 128], BF16)
        t1 = T("t1", [128, D], F32)
        pT = T("pT", [128, 8, 128], BF16)
        S = [T(f"S{i}", [128, D], F32) for i in range(2)]
        Sbf = [T(f"Sbf{i}", [128, D], BF16) for i in range(2)]
        ps = es.enter_context(nc.psum_tensor("ps", [128, 8 * 512], F32))

        ufin = T("ufin", [128, D], F32)

        B = {}

        def bf(name):
            if name not in B:
                B[name] = Buf(name)
            return B[name]

        bWin, bWout, bWpool, bWup, bconsts = bf("Win"), bf("Wout"), bf("Wpool"), bf("Wup"), bf("consts")
        bgpost, bgpre, bsout = bf("gpost"), bf("gpre"), bf("sout")
        bxb = [bf(f"xb{i}") for i in range(3)]
        bh = bf("h")
        bhT = [bf(f"hT{i}") for i in range(2)]
        bu = [bf(f"u{i}") for i in range(3)]
        bsgp = [bf(f"sgp{i}") for i in range(2)]
        bv = [bf(f"v{i}") for i in range(2)]
        bsgg = [bf(f"sgg{i}") for i in range(2)]
        bqk = bf("qk")
        bglrT = [bf(f"glrT{i}") for i in range(2)]
        bexpb, bexpnb, blsp = bf("expb"), bf("expnb"), bf("lsp")
        bqkT, bsc, bycat, bycatT, bt1, bpT = bf("qkT"), bf("sc"), bf("ycat"), bf("ycatT"), bf("t1"), bf("pT")
        bufin = bf("ufin")
        bS = [bf(f"S{i}") for i in range(2)]
        bSbf = [bf(f"Sbf{i}") for i in range(2)]
        bbank = [bf(f"bank{i}") for i in range(8)]
        V_NH = 0
        V_ONE = 32
        V_SSX, V_RSX = 4, 5
        V_SSO, V_RSO = 8, 12
        V_SSY, V_RSY = 6, 7
        V_EB0 = 16
        bssx, brsx, bsso, brso, bssy, brsy = (bf("ssx"), bf("rsx"), bf("sso"), bf("rso"),
                                              bf("ssy"), bf("rsy"))
        beb2 = [bf("eb0"), bf("eb1")]
        bvconst = bf("vconst")

        def cblk(i):
            return consts[:, i * 128:(i + 1) * 128]

        ident = cblk(0)

        def bank(b, n=1):
            return ps[:, b * 512:(b + n) * 512]

        def bank_bf(b, n=1):
            return ps[:, b * 512:(b + n) * 512].bitcast(BF16)

        P.dma("c0", lambda e: e.dma_start(out=consts[:], in_=consts_d[:, :]), writes=[bconsts])
        P.dma("c1", lambda e: e.dma_start(out=gpre[:], in_=gpre_d[:, :]), writes=[bgpre])
        P.dma("c2", lambda e: e.dma_start(out=sout[:], in_=sout_d[:, :]), writes=[bsout])

        def init_vec(e):
            e.memset(vec[:, V_ONE:V_ONE + 1], 1.0)
            return e.memset(vec[:, V_NH:V_NH + 4], -0.5)
        P.op("pool", init_vec, writes=[bvconst])
        for i in range(2):
            P.op("pool", lambda e, i=i: e.memset(glrT[i][:], 0.0), writes=[bglrT[i]])
            P.op("pool", lambda e, i=i: e.memset(glrT[i][96:128, :], 1.0), writes=[bglrT[i]])
        bWing = [bf(f"Win_g{i}") for i in range(6)]
        P.op("pool", lambda e: e.memset(WinLR[:], 0.0), writes=[bWing[5]])

        ycat_f = ycat[:].bitcast(F32)
        ycatT_f = ycatT[:].rearrange("p a b -> p (a b)").bitcast(F32)
        stage_slots = [(S[0], bS[0], "sg0"), (S[1], bS[1], "sg1"), (t1, bt1, "sg2"), (gpost, bgpost, "sg3"),
                       (xb[2], bxb[2], "xl2"), (ycat_f, bycat, "sg4"), (ycatT_f, bycatT, "sg5")]
        stage_i = [0]

        def stage_piece(src_ap, rows, width, dst_ap, scale_ap, scale_buf, dst_buf, slots=None, eng=None):
            i = stage_i[0]
            stage_i[0] += 1
            slots = stage_slots if slots is None else slots
            st_t, st_b, key = slots[i % len(slots)]
            if eng is None:
                eng = cfg["win_cast"][i % len(cfg["win_cast"])]
            st_ap = st_t[0:rows, 0:width]
            P.dma(key, lambda e: e.dma_start(out=st_ap, in_=src_ap), writes=[st_b])
            rd = [st_b] + ([scale_buf] if scale_buf is not None else [])
            if eng == "pool":
                sc1 = scale_ap if scale_ap is not None else 1.0
                fn = lambda e: e.tensor_scalar(out=dst_ap, in0=st_ap, scalar1=sc1, scalar2=1.0,
                                               op0=ALU.mult, op1=ALU.mult)
            elif eng == "act":
                if scale_ap is not None:
                    fn = lambda e: e.activation(out=dst_ap, in_=st_ap, func=AF.Copy, scale=scale_ap)
                else:
                    fn = lambda e: e.activation(out=dst_ap, in_=st_ap, func=AF.Copy)
            else:
                if scale_ap is not None:
                    fn = lambda e: e.tensor_scalar(out=dst_ap, in0=st_ap, scalar1=scale_ap, scalar2=None,
                                                   op0=ALU.mult)
                else:
                    fn = lambda e: e.tensor_copy(out=dst_ap, in_=st_ap)
            P.op(eng, fn, reads=rd, writes=[dst_buf])

        def win_group(g):
            c0 = g * 1024
            width = 1024 if g < 5 else 16
            for k in range(8):
                dst = Win[:, k, c0:c0 + width] if g < 5 else WinLR[:, k, 96:112]
                stage_piece(win_d[k * 128:(k + 1) * 128, c0:c0 + width], 128, width,
                            dst, gpre[:, k:k + 1], bgpre, bWing[g])

        def loader():
            if not cfg["wout_late"]:
                for g in range(4):
                    for kk in range(2):
                        stage_piece(wpool_d[g, kk * 128:(kk + 1) * 128, :], 128, 256, Wpool[:, g, kk, :], None, None, bWpool)
            if not cfg["wout_late"]:
                for j in range(16):
                    stage_piece(wout_d[j * 128:(j + 1) * 128, :], 128, D, Wout[:, j, :], sout[:, j:j + 1], bsout, bWout)
            win_group(5)
            stage_piece(wup_d[:, :], 128, 512, Wup[:, :], None, None, bWup)
            win_group(3)
            yield
            win_group(0)
            yield
            win_group(1)
            yield
            win_group(4)
            yield
            win_group(2)
            yield
            if not cfg["wout_late"]:
                P.dma("c3", lambda e: e.dma_start(out=gpost[:], in_=gpost_d[:, :]), writes=[bgpost])
            for b in range(2):
                P.dma(f"c{6 + b}", (lambda b: lambda e: e.dma_start(
                    out=S[b][:].rearrange("p (h v) -> p h v", h=4),
                    in_=sg_d[b].rearrange("h k v -> k h v")))(b), writes=[bS[b]])
                P.op("act", (lambda b: lambda e: e.activation(out=Sbf[b][:], in_=S[b][:], func=AF.Copy))(b),
                     reads=[bS[b]], writes=[bSbf[b]])
            yield

        def wout_loader():
            late_slots = [(t1, bt1, "sg2"), (gpost, bgpost, "sg3"), (ufin, bufin, "sg6")]
            for g in range(4):
                for kk in range(2):
                    stage_piece(wpool_d[g, kk * 128:(kk + 1) * 128, :], 128, 256, Wpool[:, g, kk, :], None, None, bWpool,
                                slots=late_slots, eng="pool")
            yield
            for j in range(16):
                stage_piece(wout_d[j * 128:(j + 1) * 128, :], 128, D, Wout[:, j, :], sout[:, j:j + 1], bsout, bWout,
                            slots=late_slots, eng="pool")
                yield
            P.dma("c3", lambda e: e.dma_start(out=gpost[:], in_=gpost_d[:, :]), writes=[bgpost])
            yield

        P.op("dve", lambda e: e.memset(t1[0:64, :], 0.0), writes=[bt1])
        P.dma("c4", lambda e: e.dma_start(out=t1[0:15, :], in_=sp_d[0, :, :]), writes=[bt1])
        P.dma("c5", lambda e: e.dma_start(out=t1[32:47, :], in_=sp_d[1, :, :]), writes=[bt1])
        P.op("dve", lambda e: e.memset(u[2][:], 0.0), writes=[bu[2]])
        P.op("dve", lambda e: e.tensor_copy(out=u[2][0:64, :], in_=t1[0:64, :]), reads=[bt1], writes=[bu[2]])
        def tile_info(n):
            if n == 0:
                return dict(sample=True, first=False, last=True, segs=[(0, 0, 64), (1, 64, 128)])
            m = (n - 1) % 16
            slot = (n - 1) // 16
            return dict(sample=False, first=(m == 0), last=(m == 15), segs=[(slot, 0, 128)], seq=2 + slot)

        def load_x(n):
            slot = n % 3
            P.dma(f"xl{slot}", lambda e: e.dma_start(out=xb[slot][:], in_=x_d[n * 128:(n + 1) * 128, :]),
                  writes=[bxb[slot]])

        def stageA_elem(n):
            slot = n % 3
            xs = xb[slot]
            P.op("act", lambda e: e.activation(out=h[:], in_=xs[:], func=AF.Square, scale=1.0 / 32.0,
                                               accum_out=vec[:, V_SSX:V_SSX + 1]),
                 reads=[bxb[slot]], writes=[bh, bssx])
            P.op("pool", lambda e: e.tensor_scalar(out=vec[:, V_RSX:V_RSX + 1], in0=vec[:, V_SSX:V_SSX + 1],
                                                   scalar1=1.0, scalar2=EPS, op0=ALU.mult, op1=ALU.add),
                 reads=[bssx], writes=[brsx])
            P.op("pool", lambda e: e.tensor_tensor(out=vec[:, V_RSX:V_RSX + 1], in0=vec[:, V_RSX:V_RSX + 1],
                                                   in1=vec[:, V_NH:V_NH + 1], op=ALU.pow),
                 reads=[brsx, bvconst], writes=[brsx])
            P.op("pool", lambda e: e.tensor_scalar(out=h[:], in0=xs[:], scalar1=vec[:, V_RSX:V_RSX + 1], scalar2=1.0,
                                                   op0=ALU.mult, op1=ALU.mult),
                 reads=[bxb[slot], brsx], writes=[bh])

        def stageA_pe(n):
            hTn = hT[n % 2]

            def fn(e):
                for c in range(8):
                    ins = e.transpose(out=bank_bf(7)[:, c * 128:(c + 1) * 128], in_=h[:, c * 128:(c + 1) * 128],
                                      identity=ident)
                return ins
            P.op("pe", fn, reads=[bh, bconsts], writes=[bbank[7]])
            evac_copy(cfg["ev_hT"], hTn[:].rearrange("p a b -> p (a b)"), bank_bf(7), [bbank[7]], [bhT[n % 2]])

        def evac_copy(eng, out_ap, in_ap, reads, writes):
            if eng == "act":
                P.op("act", lambda e: e.activation(out=out_ap, in_=in_ap, func=AF.Copy), reads=reads, writes=writes)
            else:
                P.op("dve", lambda e: e.tensor_copy(out=out_ap, in_=in_ap), reads=reads, writes=writes)

        rot = [0]

        def next_bank():
            b = rot[0] % 4
            rot[0] += 1
            return b

        def colblock(n, c0, width=512):
            b = next_bank()
            hTn = hT[n % 2]

            def fn(e):
                for k in range(8):
                    ins = e.matmul(bank(b)[:, 0:width], lhsT=hTn[:, k, :], rhs=Win[:, k, c0:c0 + width],
                                   start=(k == 0), stop=(k == 7))
                return ins
            P.op("pe", fn, reads=[bhT[n % 2], bWing[c0 // 1024]], writes=[bbank[b]])
            return b

        def phase1(n):
            info = tile_info(n)
            p2 = n % 2
            Lm = cblk(2) if info["sample"] else cblk(1)
            Lend = consts[:, NCB * 128 + 2:NCB * 128 + 4] if info["sample"] else consts[:, NCB * 128:NCB * 128 + 2]
            hTn = hT[n % 2]
            b = next_bank()

            def fn_lr(e, b=b):
                for k in range(8):
                    ins = e.matmul(bank(b)[:, 0:128], lhsT=WinLR[:, k, :], rhs=hTn[:, k, :],
                                   start=(k == 0), stop=(k == 7))
                return ins
            P.op("pe", fn_lr, reads=[bhT[n % 2], bWing[5]], writes=[bbank[b]])
            evac_copy(cfg["ev_glrT"], glrT[p2][96:112, :], bank(b)[96:112, 0:128], [bbank[b]], [bglrT[p2]])
            yield
            for half in range(2):
                b = colblock(n, 3072 + half * 512)
                evac_copy(cfg["ev_v"][half], v[p2][:, half * 512:(half + 1) * 512], bank(b), [bbank[b]], [bv[p2]])
                yield
            b = next_bank()
            P.op("pe", lambda e, b=b: e.matmul(bank(b), lhsT=glrT[p2][:, :], rhs=Wup[:, :], start=True, stop=True),
                 reads=[bglrT[p2], bWup], writes=[bbank[b]])
            P.op("act", lambda e, b=b: e.activation(out=expb[:], in_=bank(b), func=AF.Exp, scale=-1.0),
                 reads=[bbank[b]], writes=[bexpb])
            P.op("act", lambda e: e.activation(out=lsp[:], in_=expb[:], func=AF.Ln, bias=1.0),
                 reads=[bexpb], writes=[blsp])
            yield
            for half in range(2):
                b = colblock(n, half * 512)
                evac_copy(cfg["ev_u"][half], u[n % 3][:, half * 512:(half + 1) * 512], bank(b), [bbank[b]], [bu[n % 3]])
                if info["last"]:
                    P.op("act", lambda e, b=b, half=half: e.activation(out=ufin[:, half * 512:(half + 1) * 512],
                                                                       in_=bank(b), func=AF.Copy),
                         reads=[bbank[b]], writes=[bufin])
                yield
            if info["last"]:
                if info["sample"]:
                    P.dma("uf", lambda e: e.dma_start(out=pool_o[0, :, :], in_=ufin[49:64, :]), reads=[bufin])
                    P.dma("uf", lambda e: e.dma_start(out=pool_o[1, :, :], in_=ufin[113:128, :]), reads=[bufin])
                else:
                    sq = info["seq"]
                    P.dma("uf", lambda e: e.dma_start(out=pool_o[sq, :, :], in_=ufin[113:128, :]), reads=[bufin])
            b = next_bank()

            def fn_bT(e, b=b):
                for hh in range(4):
                    ins = e.matmul(bank(b)[:, hh * 128:(hh + 1) * 128], lhsT=lsp[:, hh * 128:(hh + 1) * 128], rhs=Lm,
                                   start=True, stop=True)
                return ins
            P.op("pe", fn_bT, reads=[blsp, bconsts], writes=[bbank[b]])
            P.op("act", lambda e, b=b: e.activation(out=expb[:], in_=bank(b), func=AF.Exp),
                 reads=[bbank[b]], writes=[bexpb])
            P.op("act", lambda e, b=b: e.activation(out=expnb[:], in_=bank(b), func=AF.Exp, scale=-1.0),
                 reads=[bbank[b]], writes=[bexpnb])
            yield
            def fn_eb(e):
                ebv = vec[:, V_EB0 + 8 * p2:V_EB0 + 8 * p2 + 8].rearrange("p (h s) -> p h s", s=2)
                exv = expb[:].rearrange("p (h i) -> p h i", h=4)
                e.tensor_copy(out=ebv[:, :, 0:1], in_=exv[:, :, 63:64])
                return e.tensor_copy(out=ebv[:, :, 1:2], in_=exv[:, :, 127:128])
            P.op("dve", fn_eb, reads=[bexpb], writes=[beb2[p2]])
            yield
            for half in range(2):
                b = next_bank()

                def fn_gpT(e, b=b, half=half):
                    for dcl in range(4):
                        dc = half * 4 + dcl
                        for k in range(8):
                            ins = e.matmul(bank(b)[:, dcl * 128:(dcl + 1) * 128],
                                           lhsT=Win[:, k, 1024 + dc * 128:1024 + (dc + 1) * 128], rhs=hTn[:, k, :],
                                           start=(k == 0), stop=(k == 7))
                    return ins
                P.op("pe", fn_gpT, reads=[bhT[n % 2], bWing[1]], writes=[bbank[b]])
                P.op("act", lambda e, b=b, half=half: e.activation(out=sgp[p2][:, half * 512:(half + 1) * 512],
                                                                   in_=bank(b), func=AF.Silu),
                     reads=[bbank[b]], writes=[bsgp[p2]])
                yield
            for half in range(2):
                b = colblock(n, 4096 + half * 512)
                P.op("act", lambda e, b=b, half=half: e.activation(out=sgg[p2][:, half * 512:(half + 1) * 512],
                                                                   in_=bank(b), func=AF.Silu),
                     reads=[bbank[b]], writes=[bsgg[p2]])
                yield
            for which, c0 in ((0, 2048), (1, 2560)):
                b = next_bank()

                def fn_qkT(e, b=b, c0=c0):
                    for hh in range(4):
                        for k in range(8):
                            ins = e.matmul(bank(b)[:, hh * 128:(hh + 1) * 128],
                                           lhsT=Win[:, k, c0 + hh * 128:c0 + (hh + 1) * 128], rhs=hTn[:, k, :],
                                           start=(k == 0), stop=(k == 7))
                    return ins
                P.op("pe", fn_qkT, reads=[bhT[n % 2], bWing[2]], writes=[bbank[b]])
                if which == 0:
                    P.op("dve", lambda e, b=b: e.scalar_tensor_tensor(
                        out=qkT[:, 0:4, :].rearrange("p a b -> p (a b)"), in0=bank(b), scalar=float(128.0 ** -0.5),
                        in1=expb[:], op0=ALU.mult, op1=ALU.mult),
                         reads=[bbank[b], bexpb], writes=[bqkT])
                else:
                    P.op("dve", lambda e, b=b: e.tensor_tensor(
                        out=qkT[:, 4:8, :].rearrange("p a b -> p (a b)"), in0=bank(b), in1=expnb[:], op=ALU.mult),
                         reads=[bbank[b], bexpnb], writes=[bqkT])
                yield

        def chain(n):
            info = tile_info(n)
            p2 = n % 2
            un = u[n % 3]
            uprev = u[(n - 1) % 3]
            def fn_kT(e):
                for hh in range(4):
                    ins = e.transpose(out=bank_bf(4)[:, hh * 128:(hh + 1) * 128], in_=qkT[:, 4 + hh, :], identity=ident)
                return ins
            P.op("pe", fn_kT, reads=[bqkT, bconsts], writes=[bbank[4]])
            evac_copy(cfg["ev_qkT"], qk[:, 512:1024], bank_bf(4)[:, 0:512], [bbank[4]], [bqk])
            yield
            maskc = cblk(4) if info["sample"] else cblk(3)

            def fn_sc(e):
                for hh in range(4):
                    ins = e.matmul(bank(5)[:, hh * 128:(hh + 1) * 128], lhsT=qkT[:, 4 + hh, :], rhs=qkT[:, hh, :],
                                   start=True, stop=True)
                return ins
            P.op("pe", fn_sc, reads=[bqkT], writes=[bbank[5]])
            P.op("dve", lambda e: e.tensor_tensor(out=sc[:], in0=bank(5).rearrange("p (a b) -> p a b", a=4),
                                                  in1=maskc.unsqueeze(1).to_broadcast([128, 4, 128]), op=ALU.mult),
                 reads=[bbank[5], bconsts], writes=[bsc])
            yield
            def fn_o(e):
                for hh in range(4):
                    out = bank(6, 2)[:, hh * 256:(hh + 1) * 256]
                    ins = e.matmul(out, lhsT=sc[:, hh, :], rhs=v[p2][:, hh * 256:(hh + 1) * 256], start=True, stop=False)
                    nseg = len(info["segs"])
                    for si, (slot, r0, r1) in enumerate(info["segs"]):
                        ins = e.matmul(bank(6, 2)[r0:r1, hh * 256:(hh + 1) * 256], lhsT=qkT[:, hh, r0:r1],
                                       rhs=Sbf[slot][:, hh * 256:(hh + 1) * 256], start=False, stop=True)
                return ins
            rd = [bsc, bv[p2], bqkT] + [bSbf[s[0]] for s in info["segs"]]
            P.op("pe", fn_o, reads=rd, writes=[bbank[6], bbank[7]])

            def fn_sqo(e):
                for hh in range(4):
                    ins = e.activation(out=ycat[:, D + hh * 256:D + (hh + 1) * 256], in_=bank(6, 2)[:, hh * 256:(hh + 1) * 256],
                                       func=AF.Square, scale=1.0 / 16.0, accum_out=vec[:, V_SSO + hh:V_SSO + hh + 1])
                return ins
            P.op("act", fn_sqo, reads=[bbank[6], bbank[7]], writes=[bycat, bsso])
            P.op("pool", lambda e: e.tensor_scalar(out=vec[:, V_RSO:V_RSO + 4], in0=vec[:, V_SSO:V_SSO + 4],
                                                   scalar1=1.0, scalar2=EPS, op0=ALU.mult, op1=ALU.add),
                 reads=[bsso], writes=[brso])
            P.op("pool", lambda e: e.tensor_tensor(out=vec[:, V_RSO:V_RSO + 4], in0=vec[:, V_RSO:V_RSO + 4],
                                                   in1=vec[:, V_NH:V_NH + 4], op=ALU.pow),
                 reads=[brso, bvconst], writes=[brso])

            yield
            if info["sample"]:
                cur0, hist0, hrows = 17, 21, (0, 64)
            elif info["first"]:
                cur0, hist0, hrows = 5, None, None
            else:
                cur0, hist0, hrows = 9, 13, (64, 128)

            def fn_band(e):
                for c in range(8):
                    g = c // 2
                    out = bank(4, 2)[:, c * 128:(c + 1) * 128]
                    ins = e.matmul(out, lhsT=un[:, c * 128:(c + 1) * 128], rhs=cblk(cur0 + g),
                                   start=True, stop=(hist0 is None and not info["first"]))
                    if info["first"]:
                        ins = e.matmul(out, lhsT=un[:, c * 128:(c + 1) * 128], rhs=cblk(25 + g),
                                       start=False, stop=True)
                    if hist0 is not None:
                        ins = e.matmul(out, lhsT=uprev[:, c * 128:(c + 1) * 128], rhs=cblk(hist0 + g),
                                       start=False, stop=True)
                return ins
            rd = [bu[n % 3], bconsts] + ([bu[(n - 1) % 3]] if hist0 is not None else [])
            P.op("pe", fn_band, reads=rd, writes=[bbank[4], bbank[5]])
            evac_copy(cfg["ev_pT"], pT[:].rearrange("p a b -> p (a b)"), bank(4, 2), [bbank[4], bbank[5]], [bpT])
            yield
            def fn_yp(e):
                for g in range(4):
                    for dh in range(2):
                        dc = 2 * g + dh
                        for kk in range(2):
                            ins = e.matmul(bank(4, 2)[:, dc * 128:(dc + 1) * 128],
                                           lhsT=Wpool[:, g, kk, dh * 128:(dh + 1) * 128], rhs=pT[:, 2 * g + kk, :],
                                           start=(kk == 0), stop=(kk == 1))
                return ins
            P.op("pe", fn_yp, reads=[bpT, bWpool], writes=[bbank[4], bbank[5]])
            P.op("dve", lambda e: e.tensor_tensor(out=ycatT[:, 0:8, :].rearrange("p a b -> p (a b)"), in0=bank(4, 2),
                                                  in1=sgp[p2][:], op=ALU.mult),
                 reads=[bbank[4], bbank[5], bsgp[p2]], writes=[bycatT])
            yield
            def fn_yg(e):
                for hh in range(4):
                    ins = e.scalar_tensor_tensor(out=ycat[:, D + hh * 256:D + (hh + 1) * 256],
                                                 in0=bank(6, 2)[:, hh * 256:(hh + 1) * 256],
                                                 scalar=vec[:, V_RSO + hh:V_RSO + hh + 1],
                                                 in1=sgg[p2][:, hh * 256:(hh + 1) * 256], op0=ALU.mult, op1=ALU.mult)
                return ins
            P.op("dve", fn_yg, reads=[bbank[6], bbank[7], brso, bsgg[p2]], writes=[bycat])
            yield
            V_EB = V_EB0 + 8 * p2
            V_EBP = V_EB0 + 8 * (1 - p2)
            for (slot, r0, r1) in info["segs"]:
                seg = 1 if not info["sample"] else slot
                fresh = info["sample"] or info["first"]

                def fn_kv(e, r0=r0, r1=r1):
                    for hh in range(4):
                        ins = e.matmul(bank(4, 2)[:, hh * 256:(hh + 1) * 256], lhsT=qk[r0:r1, 512 + hh * 128:512 + (hh + 1) * 128],
                                       rhs=v[p2][r0:r1, hh * 256:(hh + 1) * 256], start=True, stop=True)
                    return ins
                P.op("pe", fn_kv, reads=[bqk, bv[p2]], writes=[bbank[4], bbank[5]])

                def fn_su(e, slot=slot, seg=seg, fresh=fresh):
                    for hh in range(4):
                        sc_ap = (vec[:, V_ONE:V_ONE + 1] if fresh
                                 else vec[:, V_EBP + hh * 2 + 1:V_EBP + hh * 2 + 2])
                        ins = e.scalar_tensor_tensor(out=S[slot][:, hh * 256:(hh + 1) * 256],
                                                     in0=S[slot][:, hh * 256:(hh + 1) * 256],
                                                     scalar=sc_ap,
                                                     in1=bank(4, 2)[:, hh * 256:(hh + 1) * 256], op0=ALU.mult, op1=ALU.add)
                    return ins
                P.op("dve", fn_su, reads=[bbank[4], bbank[5], beb2[1 - p2], bvconst, bS[slot]], writes=[bS[slot]])

                def fn_sb(e, slot=slot, seg=seg, last=info["last"]):
                    for hh in range(4):
                        dst = S[slot] if last else Sbf[slot]
                        ins = e.tensor_scalar(out=dst[:, hh * 256:(hh + 1) * 256], in0=S[slot][:, hh * 256:(hh + 1) * 256],
                                              scalar1=vec[:, V_EB + hh * 2 + seg:V_EB + hh * 2 + seg + 1], scalar2=None,
                                              op0=ALU.mult)
                    return ins
                if info["last"]:
                    P.op("dve", fn_sb, reads=[bS[slot], beb2[p2]], writes=[bS[slot]])
                    sq = slot if info["sample"] else info["seq"]
                    P.dma(f"ss{slot}", lambda e, slot=slot, sq=sq: e.dma_start(
                        out=gla_o[sq].rearrange("h k v -> k h v"),
                        in_=S[slot][:].rearrange("p (h v) -> p h v", h=4)), reads=[bS[slot]])
                    if info["sample"]:
                        P.op("dve", lambda e, slot=slot: e.memset(S[slot][:], 0.0), writes=[bS[slot]])
                        P.op("dve", lambda e, slot=slot: e.memset(Sbf[slot][:], 0.0), writes=[bSbf[slot]])
                else:
                    P.op("dve", fn_sb, reads=[bS[slot], beb2[p2]], writes=[bSbf[slot]])
            yield
            def fn_yT(e):
                for j in range(8):
                    ins = e.transpose(out=bank_bf(6)[:, j * 128:(j + 1) * 128], in_=ycat[:, D + j * 128:D + (j + 1) * 128],
                                      identity=ident)
                return ins
            P.op("pe", fn_yT, reads=[bycat, bconsts], writes=[bbank[6]])
            evac_copy(cfg["ev_ycatT"], ycatT[:, 8:16, :].rearrange("p a b -> p (a b)"), bank_bf(6), [bbank[6]], [bycatT])
            yield
            def fn_out(e):
                for half in range(2):
                    for j in range(16):
                        ins = e.matmul(bank(4 + half), lhsT=ycatT[:, j, :], rhs=Wout[:, j, half * 512:(half + 1) * 512],
                                       start=(j == 0), stop=(j == 15))
                return ins
            P.op("pe", fn_out, reads=[bycatT, bWout], writes=[bbank[4], bbank[5]])
            P.op("act", lambda e: e.activation(out=ycatT[:].rearrange("p a b -> p (a b)")[:, 0:D], in_=bank(4, 2),
                                               func=AF.Square, scale=1.0 / 32.0, accum_out=vec[:, V_SSY:V_SSY + 1]),
                 reads=[bbank[4], bbank[5]], writes=[bycatT, bssy])
            P.op("pool", lambda e: e.tensor_scalar(out=vec[:, V_RSY:V_RSY + 1], in0=vec[:, V_SSY:V_SSY + 1],
                                                   scalar1=1.0, scalar2=EPS, op0=ALU.mult, op1=ALU.add),
                 reads=[bssy], writes=[brsy])
            P.op("pool", lambda e: e.tensor_tensor(out=vec[:, V_RSY:V_RSY + 1], in0=vec[:, V_RSY:V_RSY + 1],
                                                   in1=vec[:, V_NH:V_NH + 1], op=ALU.pow),
                 reads=[brsy, bvconst], writes=[brsy])
            yield
            P.op("dve", lambda e: e.scalar_tensor_tensor(out=t1[:], in0=bank(4, 2), scalar=vec[:, V_RSY:V_RSY + 1],
                                                         in1=gpost[:], op0=ALU.mult, op1=ALU.mult),
                 reads=[bbank[4], bbank[5], brsy, bgpost], writes=[bt1])
            slot = n % 3
            P.op("pool", lambda e: e.tensor_tensor(out=xb[slot][:], in0=xb[slot][:], in1=t1[:], op=ALU.add),
                 reads=[bxb[slot], bt1], writes=[bxb[slot]])
            P.dma(f"xs{slot}", lambda e: e.dma_start(out=y_d[n * 128:(n + 1) * 128, :], in_=xb[slot][:]),
                  reads=[bxb[slot]])
            if n + 3 < nt:
                load_x(n + 3)
            yield

        load_x(0)

        def run_all(g):
            for _ in g:
                pass

        if True:
            SCHED = cfg["sched"]
            ld = loader()
            if nt > 1:
                load_x(1)
            next(ld)
            stageA_elem(0)
            stageA_pe(0)
            g1 = phase1(0)
            blk = 0
            for nblk in (4, 4, 2, 2, 2):
                for _ in range(nblk):
                    next(g1)
                    if blk == cfg["stageA_elem"] and nt > 1:
                        stageA_elem(1)
                    if blk == cfg["stageA_pe"] and nt > 1:
                        stageA_pe(1)
                    blk += 1
                next(ld)
            run_all(g1)
            run_all(ld)
            if nt > 2:
                load_x(2)
            prev = chain(0)
            wl = wout_loader() if cfg["wout_late"] else iter(())
            carry = None
            defer = cfg["wout_late"] and cfg["defer_out0"] and nt > 2
            for n in range(1, nt):
                g1 = phase1(n)
                g2 = prev if prev is not None else iter(())
                SCHED = cfg["sched1"] if (n == 1 and cfg["wout_late"]) else cfg["sched"]
                sa_elem = cfg["stageA_elem"]
                if n == 2 and carry is not None:
                    sa_elem = cfg["stageA_elem2"]
                blk = 0
                while True:
                    try:
                        next(g1)
                    except StopIteration:
                        break
                    for _ in range(2):
                        try:
                            next(wl)
                        except StopIteration:
                            pass
                    if blk == sa_elem and n + 1 < nt:
                        stageA_elem(n + 1)
                    if blk == cfg["stageA_pe"] and n + 1 < nt:
                        stageA_pe(n + 1)
                    for _ in range(SCHED.get(blk, 0)):
                        try:
                            next(g2)
                        except StopIteration:
                            pass
                    if n == 2 and carry is not None and blk == cfg["carry_blk"]:
                        run_all(carry)
                        carry = None
                    blk += 1
                run_all(wl)
                if n == 1 and defer:
                    carry = g2
                else:
                    run_all(g2)
                prev = chain(n)
            run_all(prev)

        P.resolve()
        if dry:
            return P
        esem = {e: es.enter_context(nc.semaphore(f"sem_{e}")) for e in ("pe", "act", "dve", "pool")}
        dsem = {k: es.enter_context(nc.semaphore(f"dsem_{k}")) for k in P.dma_cnt}
        with nc.Block() as block:
            @block.sync
            def _(eng):
                P.emit("sp", eng, esem, dsem)

            @block.tensor
            def _(eng):
                P.emit("pe", eng, esem, dsem)

            @block.scalar
            def _(eng):
                P.emit("act", eng, esem, dsem)

            @block.vector
            def _(eng):
                P.emit("dve", eng, esem, dsem)

            @block.gpsimd
            def _(eng):
                P.emit("pool", eng, esem, dsem)
    return nc


_CONSTS = None


def kernel(x_prompt, x_sample, state_pool, state_gla, g_pre, w_in, w_gate_up, b_gate_up,
           w_pool, pool_scale, g_gla_out, w_out, g_post):
    global _CONSTS
    if _CONSTS is None:
        _CONSTS = _build_consts()
    f = lambda a: np.ascontiguousarray(np.asarray(a, dtype=np.float32))
    x_prompt, x_sample, state_pool, state_gla = f(x_prompt), f(x_sample), f(state_pool), f(state_gla)
    g_pre, w_in, w_gate_up, b_gate_up = f(g_pre), f(w_in), f(w_gate_up), f(b_gate_up)
    w_pool, pool_scale, g_gla_out, w_out, g_post = f(w_pool), f(pool_scale), f(g_gla_out), f(w_out), f(g_post)

    wup = np.zeros((128, 512), np.float32)
    wup[96:112] = w_gate_up[0]
    wup[112] = b_gate_up[0]
    gpre = np.ascontiguousarray(g_pre[0].reshape(8, 128).T)
    srow = np.concatenate([pool_scale[0], np.tile(g_gla_out[0], 4)])
    sout = np.ascontiguousarray(srow.reshape(16, 128).T)
    gpost = np.ascontiguousarray(np.broadcast_to(g_post[0][None, :], (128, D)))

    in_maps = []
    for c in range(8):
        xs = np.concatenate([x_sample[2 * c:2 * c + 2].reshape(128, D),
                             x_prompt[2 * c].reshape(2048, D),
                             x_prompt[2 * c + 1].reshape(2048, D)], axis=0)
        in_maps.append({
            "x": np.ascontiguousarray(xs),
            "state_pool": np.ascontiguousarray(state_pool[0, 2 * c:2 * c + 2]),
            "state_gla": np.ascontiguousarray(state_gla[0, 2 * c:2 * c + 2]),
            "w_in": w_in[0], "w_out": w_out[0], "w_pool": w_pool[0], "wup": wup,
            "gpre": gpre, "sout": sout, "gpost": gpost, "consts": _CONSTS,
        })
    nc = build_program()
    res = run_bass_kernel_spmd(nc, in_maps, core_ids=list(range(8)))
    y_prompt = np.empty((16, 2048, D), np.float32)
    y_sample = np.empty((16, 64, D), np.float32)
    new_pool_prompt = np.empty((1, 16, 15, D), np.float32)
    new_gla_prompt = np.empty((1, 16, 4, 128, 256), np.float32)
    new_pool_sample = np.empty((1, 16, 15, D), np.float32)
    new_gla_sample = np.empty((1, 16, 4, 128, 256), np.float32)
    for c in range(8):
        r = res.results[c]
        y = r["y"]
        y_sample[2 * c:2 * c + 2] = y[0:128].reshape(2, 64, D)
        y_prompt[2 * c] = y[128:128 + 2048]
        y_prompt[2 * c + 1] = y[128 + 2048:128 + 4096]
        po, go = r["pool_out"], r["gla_out"]
        new_pool_sample[0, 2 * c:2 * c + 2] = po[0:2]
        new_pool_prompt[0, 2 * c:2 * c + 2] = po[2:4]
        new_gla_sample[0, 2 * c:2 * c + 2] = go[0:2]
        new_gla_prompt[0, 2 * c:2 * c + 2] = go[2:4]
    return (y_prompt, y_sample, new_pool_prompt, new_gla_prompt, new_pool_sample, new_gla_sample)
```
